# Optimizing a Trainium2 kernel written in Bass

```python
import math
import jax, jax.numpy as jnp
from jax import lax
import numpy as np

D_MODEL = 1024
BATCH = 4
SEQ = 8192
DEPTH = 2

CHUNK = 64
DN_HEADS = 4
DN_DK = 128
DN_DV = 128
DN_CONV = 4
SG_GROUPS = 4
SG_GROUP_DIM = 128
SG_BLOCK = 128
FFN_DIM = 2816
FFN_CONV = 3

LN_EPS = 1e-5
RMS_EPS = 1e-6
L2_EPS = 1e-6
DEEPNORM_ALPHA = (2 * DEPTH) ** 0.25
DEEPNORM_BETA = (8 * DEPTH) ** -0.25

QK_W = DN_HEADS * DN_DK
V_W = DN_HEADS * DN_DV
SG_W = SG_GROUPS * SG_GROUP_DIM
IN_SPLIT_SIZES = (2 * QK_W + V_W, V_W, DN_HEADS, DN_HEADS, 2 * SG_W, 2 * D_MODEL)
IN_COLS = sum(IN_SPLIT_SIZES)

kernel_name = "hybrid_deltanet_spatialgate_convffn_deepnorm"


def _split_points():
    pts, acc = [], 0
    for s in IN_SPLIT_SIZES[:-1]:
        acc += s
        pts.append(acc)
    return pts


def layer_norm(x, g, b):
    xf = x.astype(jnp.float32)
    mu = jnp.mean(xf, axis=-1, keepdims=True)
    var = jnp.mean(jnp.square(xf - mu), axis=-1, keepdims=True)
    y = (xf - mu) * lax.rsqrt(var + LN_EPS) * g.astype(jnp.float32) + b.astype(jnp.float32)
    return y.astype(x.dtype)


def causal_depthwise_conv(x, w):
    K, C = w.shape
    return lax.conv_general_dilated(
        x, w[:, None, :].astype(x.dtype), window_strides=(1,), padding=[(K - 1, 0)],
        dimension_numbers=("NWC", "WIO", "NWC"), feature_group_count=C)


def l2norm(x):
    return x * lax.rsqrt(jnp.sum(x * x, axis=-1, keepdims=True) + L2_EPS)


def _to_chunks(t, n):
    b = t.shape[0]
    t = t.reshape((b, n, CHUNK) + t.shape[2:])
    return jnp.swapaxes(t, 2, 3)


def gated_delta_rule(q, k, v, beta, g):
    B, S, H, Dk = q.shape
    Dv = v.shape[-1]
    n = S // CHUNK
    q, k, v = _to_chunks(q, n), _to_chunks(k, n), _to_chunks(v, n)
    beta, g = _to_chunks(beta, n), _to_chunks(g, n)
    G = jnp.cumsum(g, axis=-1)
    causal = jnp.tril(jnp.ones((CHUNK, CHUNK), dtype=bool))
    strict = jnp.tril(jnp.ones((CHUNK, CHUNK), dtype=bool), k=-1)
    diff = G[..., :, None] - G[..., None, :]
    decay = jnp.exp(jnp.where(causal, diff, -jnp.inf))
    kb = k * beta[..., None]
    L = jnp.where(strict, jnp.einsum("bnhid,bnhjd->bnhij", kb, k) * decay, 0.0)
    eye = jnp.eye(CHUNK, dtype=jnp.float32)
    T = lax.linalg.triangular_solve(eye + L, jnp.broadcast_to(eye, L.shape),
                                    left_side=True, lower=True)
    W = jnp.einsum("bnhij,bnhjd->bnhid", T, kb * jnp.exp(G)[..., None])
    U = jnp.einsum("bnhij,bnhjd->bnhid", T, v * beta[..., None])
    A_qk = jnp.einsum("bnhid,bnhjd->bnhij", q, k) * decay
    q_g = q * jnp.exp(G)[..., None]
    G_last = G[..., -1]
    k_d = k * jnp.exp(G_last[..., None] - G)[..., None]
    g_last = jnp.exp(G_last)

    def step(state, inp):
        qg, kd, w, u, aqk, gl = inp
        u_new = u - jnp.einsum("bhck,bhkv->bhcv", w, state)
        o = jnp.einsum("bhck,bhkv->bhcv", qg, state) + jnp.einsum("bhij,bhjv->bhiv", aqk, u_new)
        state = state * gl[..., None, None] + jnp.einsum("bhck,bhcv->bhkv", kd, u_new)
        return state, o

    xs = tuple(jnp.moveaxis(t, 1, 0) for t in (q_g, k_d, W, U, A_qk, g_last))
    s0 = jnp.zeros((B, H, Dk, Dv), jnp.float32)
    _, o = lax.scan(step, s0, xs)
    o = jnp.swapaxes(jnp.moveaxis(o, 0, 1), 2, 3)
    return o.reshape(B, S, H, Dv)


def deltanet_branch(qkv, z, beta_logit, a, conv_w, a_log, dt_bias, norm_w):
    B, S, _ = qkv.shape
    qkv = jax.nn.silu(causal_depthwise_conv(qkv, conv_w))
    q, k, v = jnp.split(qkv, [QK_W, 2 * QK_W], axis=-1)
    f32 = jnp.float32
    q = l2norm(q.reshape(B, S, DN_HEADS, DN_DK).astype(f32)) * (DN_DK ** -0.5)
    k = l2norm(k.reshape(B, S, DN_HEADS, DN_DK).astype(f32))
    v = v.reshape(B, S, DN_HEADS, DN_DV).astype(f32)
    beta = jax.nn.sigmoid(beta_logit.astype(f32))
    g = -jnp.exp(a_log.astype(f32)) * jax.nn.softplus(a.astype(f32) + dt_bias.astype(f32))
    o = gated_delta_rule(q, k, v, beta, g)
    o = o * lax.rsqrt(jnp.mean(o * o, axis=-1, keepdims=True) + RMS_EPS) * norm_w.astype(f32)
    o = o * jax.nn.silu(z.reshape(B, S, DN_HEADS, DN_DV).astype(f32))
    return o.reshape(B, S, V_W).astype(qkv.dtype)


def spatial_gating_branch(uv, ln_g, ln_b, w_s, b_s):
    B, S, _ = uv.shape
    u, v = jnp.split(jax.nn.gelu(uv), 2, axis=-1)
    v = layer_norm(v, ln_g, ln_b)
    n = S // SG_BLOCK
    v = v.reshape(B, n, SG_BLOCK, SG_GROUPS, SG_GROUP_DIM)
    mask = jnp.tril(jnp.ones((SG_BLOCK, SG_BLOCK), dtype=bool))
    w = jnp.where(mask, w_s, 0.0).astype(v.dtype)
    mixed = jnp.einsum("gpq,bnqgc->bnpgc", w, v) + b_s.T[:, :, None].astype(v.dtype)
    return u * mixed.reshape(B, S, SG_W)


def setup_inputs(seed: int = 0) -> dict:
    key = jax.random.key(seed)
    ks = jax.random.split(key, 24)
    nrm = jax.random.normal
    D = D_MODEL
    x = nrm(ks[0], (BATCH, SEQ, D), jnp.float32)
    w_in = nrm(ks[1], (DEPTH, D, IN_COLS), jnp.float32) * D ** -0.5
    conv_qkv = nrm(ks[2], (DEPTH, DN_CONV, 2 * QK_W + V_W), jnp.float32) * DN_CONV ** -0.5
    a_log = jnp.log(jax.random.uniform(ks[3], (DEPTH, DN_HEADS), jnp.float32, 1.0, 16.0))
    dt = jnp.exp(jax.random.uniform(ks[4], (DEPTH, DN_HEADS), jnp.float32,
                                    math.log(1e-3), math.log(1e-1)))
    dt_bias = dt + jnp.log(-jnp.expm1(-dt))
    dn_norm_w = 1.0 + 0.02 * nrm(ks[5], (DEPTH, DN_DV), jnp.float32)
    w_branch_a = nrm(ks[6], (DEPTH, V_W, D), jnp.float32) * V_W ** -0.5 * DEEPNORM_BETA
    sg_ln_g = 1.0 + 0.02 * nrm(ks[7], (DEPTH, SG_W), jnp.float32)
    sg_ln_b = 0.02 * nrm(ks[8], (DEPTH, SG_W), jnp.float32)
    w_spatial = nrm(ks[9], (DEPTH, SG_GROUPS, SG_BLOCK, SG_BLOCK), jnp.float32) * SG_BLOCK ** -0.5
    b_spatial = 1.0 + 0.02 * nrm(ks[10], (DEPTH, SG_GROUPS, SG_BLOCK), jnp.float32)
    w_branch_b = nrm(ks[11], (DEPTH, SG_W, D), jnp.float32) * SG_W ** -0.5 * DEEPNORM_BETA
    w_out = nrm(ks[12], (DEPTH, D, D), jnp.float32) * D ** -0.5 * DEEPNORM_BETA
    ln1_g = 1.0 + 0.02 * nrm(ks[13], (DEPTH, D), jnp.float32)
    ln1_b = 0.02 * nrm(ks[14], (DEPTH, D), jnp.float32)
    w_up = nrm(ks[15], (DEPTH, D, 2 * FFN_DIM), jnp.float32) * D ** -0.5
    conv_ffn = nrm(ks[16], (DEPTH, FFN_CONV, 2 * FFN_DIM), jnp.float32) * FFN_CONV ** -0.5
    w_down = nrm(ks[17], (DEPTH, FFN_DIM, D), jnp.float32) * FFN_DIM ** -0.5 * DEEPNORM_BETA
    ln2_g = 1.0 + 0.02 * nrm(ks[18], (DEPTH, D), jnp.float32)
    ln2_b = 0.02 * nrm(ks[19], (DEPTH, D), jnp.float32)
    return {"x": x, "w_in": w_in, "conv_qkv": conv_qkv, "a_log": a_log, "dt_bias": dt_bias,
            "dn_norm_w": dn_norm_w, "w_branch_a": w_branch_a, "sg_ln_g": sg_ln_g,
            "sg_ln_b": sg_ln_b, "w_spatial": w_spatial, "b_spatial": b_spatial,
            "w_branch_b": w_branch_b, "w_out": w_out, "ln1_g": ln1_g, "ln1_b": ln1_b,
            "w_up": w_up, "conv_ffn": conv_ffn, "w_down": w_down, "ln2_g": ln2_g, "ln2_b": ln2_b}


def reference(x, w_in, conv_qkv, a_log, dt_bias, dn_norm_w, w_branch_a, sg_ln_g, sg_ln_b,
              w_spatial, b_spatial, w_branch_b, w_out, ln1_g, ln1_b, w_up, conv_ffn, w_down,
              ln2_g, ln2_b):
    pts = _split_points()
    for l in range(DEPTH):
        proj = x @ w_in[l]
        qkv, z, beta_logit, a, sg_uv, gates = jnp.split(proj, pts, axis=-1)
        o_a = deltanet_branch(qkv, z, beta_logit, a, conv_qkv[l], a_log[l], dt_bias[l], dn_norm_w[l])
        o_b = spatial_gating_branch(sg_uv, sg_ln_g[l], sg_ln_b[l], w_spatial[l], b_spatial[l])
        gate_a, gate_b = jnp.split(jax.nn.sigmoid(gates), 2, axis=-1)
        h = gate_a * (o_a @ w_branch_a[l]) + gate_b * (o_b @ w_branch_b[l])
        x = layer_norm(DEEPNORM_ALPHA * x + h @ w_out[l], ln1_g[l], ln1_b[l])
        up = causal_depthwise_conv(x @ w_up[l], conv_ffn[l])
        a_ff, b_ff = jnp.split(up, 2, axis=-1)
        x = layer_norm(DEEPNORM_ALPHA * x + (jax.nn.silu(a_ff) * b_ff) @ w_down[l], ln2_g[l], ln2_b[l])
    return x
```

```python
import numpy as np
from contextlib import ExitStack
import concourse.bass as bass
import concourse.mybir as mybir
from concourse.bass_utils import run_bass_kernel_spmd

F32 = mybir.dt.float32
BF16 = mybir.dt.bfloat16
AF = mybir.ActivationFunctionType
ALU = mybir.AluOpType

P = 128
T = 512
D = 1024
KC = 8
DEPTH = 2
SEQ = 8192
BATCH = 4
FFN = 2816
NBLK = 31
B_IN, B_AB, B_OUT, B_UP, B_DN = 0, 10, 12, 14, 25
ALPHA = (2 * DEPTH) ** 0.25
LN_EPS = 1e-5
RMS_EPS = 1e-6
L2_EPS = 1e-6
BIG = 32768.0
CQ, CF, NW, L1G, L1B, L2G, L2B, ALOG, DTB, NPC = 0, 48, 180, 181, 189, 197, 205, 213, 217, 221
C_ID, C_TRIU, C_BLK, C_MUS, C_MUI, C_NLS, C_SPU, C_ONE = range(8)
GELU_C = 1.5957691216057308


class _Op:
    __slots__ = ("eng", "fn", "deps", "signal", "tick", "semkey", "isdma", "raw", "inc")

    def __init__(self, eng, fn, isdma, semkey, inc=None):
        self.inc = inc if inc is not None else (16 if isdma else 1)
        self.eng = eng
        self.fn = fn
        self.deps = []
        self.signal = False
        self.tick = 0
        self.semkey = semkey
        self.isdma = isdma


class Sched:
    ENGS = ("pe", "act", "dve", "pool", "sp")

    def __init__(self):
        self.ops = {e: [] for e in self.ENGS}
        self.last_w = {}
        self.readers = {}
        self.all = []
        self.alias = {}

    def add(self, eng, fn, reads=(), writes=(), dma=False, semkey=None, inc=None):
        op = _Op(eng, fn, dma, semkey if dma else eng, inc)
        reads = [self.alias.get(k, k) for k in reads]
        writes = [self.alias.get(k, k) for k in writes]
        deps = {}
        for k in reads:
            w = self.last_w.get(k)
            if w is not None:
                deps[id(w)] = (w, True)
        for k in writes:
            w = self.last_w.get(k)
            if w is not None and id(w) not in deps:
                deps[id(w)] = (w, False)
            for r in self.readers.get(k, ()):
                if id(r) not in deps:
                    deps[id(r)] = (r, False)
        for d, israw in deps.values():
            if d.isdma:
                op.deps.append(d)
            elif d.eng == eng and not dma:
                if eng != "pe" and israw:
                    op.deps.append(d)
            else:
                op.deps.append(d)
        for d in op.deps:
            d.signal = True
        for k in reads:
            self.readers.setdefault(k, []).append(op)
        for k in writes:
            self.last_w[k] = op
            self.readers[k] = []
        self.ops[eng].append(op)
        self.all.append(op)
        return op

    def emit(self, nc, es):
        counts = {}
        for op in self.all:
            if op.signal:
                counts[op.semkey] = counts.get(op.semkey, 0) + op.inc
                op.tick = counts[op.semkey]
        sems = {}
        for k in counts:
            sems[k] = es.enter_context(nc.semaphore("s_" + str(k)))
        block = es.enter_context(nc.Block())

        def run(engname):
            def body(e):
                waited = {}
                for op in self.ops[engname]:
                    need = {}
                    for d in op.deps:
                        if need.get(d.semkey, 0) < d.tick:
                            need[d.semkey] = d.tick
                    for k, v in need.items():
                        if waited.get(k, 0) < v:
                            e.wait_ge(sems[k], v)
                            waited[k] = v
                    ins = op.fn(e)
                    if op.signal:
                        if op.isdma and op.inc == 1:
                            ins.then_inc(sems[op.semkey])
                        else:
                            ins.then_inc(sems[op.semkey], op.inc)
            return body

        block.tensor(run("pe"))
        block.scalar(run("act"))
        block.vector(run("dve"))
        block.gpsimd(run("pool"))
        block.sync(run("sp"))


class Builder:
    def __init__(self, n_steps, depth=1, pp=False):
        self.n_steps = n_steps
        self.depth = depth
        self.pp = pp
        self.pending_cc = None
        self.deferred = None
        self.nc = bass.Bass("TRN2", target_bir_lowering=False)
        self.s = Sched()
        self.psn = 0
        self.wn = 0
        self.rawn = 0

    def mm(self, out, lhsT, rhs, start, stop, r, w):
        self.s.add("pe", lambda e: e.matmul(out, lhsT=lhsT, rhs=rhs, start=start, stop=stop), r, w)

    def tr(self, out, in_, ident, r, w):
        self.s.add("pe", lambda e: e.transpose(out, in_, ident), r, w)

    def act(self, out, in_, func, r, w, bias=None, scale=None):
        kw = {}
        if bias is not None:
            kw["bias"] = bias
        if scale is not None:
            kw["scale"] = scale
        self.s.add("act", lambda e: e.activation(out=out, in_=in_, func=func, **kw), r, w)

    def tt(self, eng, out, in0, in1, alu, r, w):
        self.s.add(eng, lambda e: e.tensor_tensor(out=out, in0=in0, in1=in1, op=alu), r, w)

    def ts(self, eng, out, in0, s1, s2, op0, op1, r, w):
        if op1 is None:
            self.s.add(eng, lambda e: e.tensor_scalar(out=out, in0=in0, scalar1=s1, scalar2=None, op0=op0), r, w)
        else:
            self.s.add(eng, lambda e: e.tensor_scalar(out=out, in0=in0, scalar1=s1, scalar2=s2, op0=op0, op1=op1), r, w)

    def stt(self, eng, out, in0, sc, in1, op0, op1, r, w):
        eng = "dve"
        self.s.add(eng, lambda e: e.scalar_tensor_tensor(out=out, in0=in0, scalar=sc, in1=in1, op0=op0, op1=op1), r, w)

    def cp(self, eng, out, in_, r, w):
        if eng == "act":
            self.s.add("act", lambda e: e.activation(out=out, in_=in_, func=AF.Copy), r, w)
        else:
            self.s.add(eng, lambda e: e.tensor_copy(out=out, in_=in_), r, w)

    def memset(self, eng, ap, val, w):
        self.s.add(eng, lambda e: e.memset(ap, val), (), w)

    def dma(self, eng, out, in_, r, w, semkey):
        self.s.add(eng, lambda e: e.dma_start(out=out, in_=in_), r, w, dma=True, semkey=semkey)

    def ps(self):
        n = getattr(self, "ps_n", 8)
        b = self.psn % n
        self.psn += 1
        return b

    def build(self):
        nc = self.nc
        n_steps = self.n_steps
        ntok = n_steps * T
        dt = nc.dram_tensor
        self.xin = dt("xin", [D, ntok], F32, kind="ExternalInput").ap()
        self.wblk_l = [dt("wblk%d" % l, [NBLK, P, 4096], F32, kind="ExternalInput").ap() for l in range(self.depth)]
        self.wba_l = [dt("wba%d" % l, [P, 64], F32, kind="ExternalInput").ap() for l in range(self.depth)]
        self.pcol_dl = [dt("pcol%d" % l, [P, NPC], F32, kind="ExternalInput").ap() for l in range(self.depth)]
        self.prow_dl = [dt("prow%d" % l, [P, 1536], F32, kind="ExternalInput").ap() for l in range(self.depth)]
        self.wsT_dl = [dt("wsT%d" % l, [P, 512], F32, kind="ExternalInput").ap() for l in range(self.depth)]
        self.cst_d = dt("cst", [P, 1024], F32, kind="ExternalInput").ap()
        self.yout = dt("yout", [D, ntok], F32, kind="ExternalOutput").ap()
        if self.pp:
            self.flag_d = dt("flag", [P, 2 + n_steps], F32, kind="ExternalInput").ap()
            self.snd = [nc.dram_tensor("snd%d" % i, [D, T], F32) for i in range(2)]
            self.gat = [nc.dram_tensor("gat%d" % i, [2 * D, T], F32) for i in range(2)]
        with ExitStack() as es:
            self.es = es
            self.alloc()
            self.prologue()
            for s in range(n_steps):
                for l in range(self.depth):
                    self.step(s, l)
            self.s.add("sp", lambda e: e.nop(), ["yout"], ())
            self.s.emit(nc, es)
        return nc

    def sb(self, name, shape, dtype):
        return self.es.enter_context(self.nc.sbuf_tensor(name, shape, dtype))

    def alloc(self):
        sb = self.sb
        self.xT = sb("xT", [P, 8, T], F32)
        self.y = sb("y", [P, 8, T], F32)
        self.actb = sb("actb", [P, 8, T], BF16)
        self.NW = 3
        self.wring = [sb("wr%d" % i, [P, 8, 512], BF16) for i in range(self.NW)]
        self.raw = [sb("raw%d" % i, [P, 516], F32) for i in range(3)]
        self.acc = [sb("acc%d" % i, [P, T], F32) for i in range(3)]
        self.qkv = sb("qkv", [P, 12, T], F32)
        self.zs = sb("zs", [P, 4, T], BF16)
        self.ug = sb("ug", [P, 4, T], BF16)
        self.gates = sb("gates", [P, 16, T], BF16)
        self.sgv = sb("sgv", [P, 4, T], BF16)
        self.osb = sb("osb", [P, 4, T], F32)
        self.hid = self.qkv[:].rearrange("p a b -> p (a b)").bitcast(BF16).rearrange("p (a b) -> p a b", b=T)
        self.oab = sb("oab", [P, 8, T], BF16)
        self.tmp = [sb("tmp%d" % i, [P, T], F32) for i in range(4)]
        self.lnb = sb("lnb", [P, 4, T], BF16)
        self.tmpb = [self.lnb[:, i, :] for i in range(2)]
        self.s.alias["tmpb0"] = "lnb0"
        self.s.alias["tmpb1"] = "lnb1"
        self.stat = self.osb
        self.haloq_l = [sb("haloq%d" % l, [P, 12, 4], F32) for l in range(self.depth)]
        self.halof_l = [sb("halof%d" % l, [P, 44, 4], F32) for l in range(self.depth)]
        self.S_l = [sb("S%d" % l, [P, 4, P], F32) for l in range(self.depth)]
        self.ba = sb("ba", [P, 4, 8], F32)
        self.cols = sb("cols", [P, 12, 4, 4], F32)
        self.dn = {n: sb("dn_" + n, [P, 4, P], F32) for n in ["DUs", "DUi", "DLs", "EG"]}
        for n in ["Y", "YT", "Pm", "Z", "ZT", "Z2", "ZT2", "Kbg", "Vb"]:
            self.dn[n] = sb("dn_" + n, [P, 4, P], BF16)
        self.qkb = sb("qkb", [P, 8, P], BF16)
        self.dn2 = []
        for i in range(2):
            d = {n: sb("dn%d_%s" % (i, n), [P, 4, P], BF16) for n in ["WT", "qg", "Aqk", "kd", "un"]}
            d["U"] = sb("dn%d_U" % i, [P, 4, P], F32)
            d["gl"] = sb("dn%d_gl" % i, [P, 4, 2], F32)
            self.dn2.append(d)
        self.Sb_l = [sb("Sb%d" % l, [P, 4, P], BF16) for l in range(self.depth)]
        self.small = sb("small", [P, 16, 8], F32)
        self.epsc = sb("epsc", [P, 4], F32)
        if self.pp:
            self.flg = sb("flg", [P, 2 + self.n_steps], F32)
        self.pcol_l = [sb("pcolsb%d" % l, [P, NPC], F32) for l in range(self.depth)]
        self.prow = sb("prowsb", [P, 1536], F32)
        self.cst = sb("cstsb", [P, 3, P], F32)
        self.cstb = sb("cstb", [P, 8, P], BF16)
        self.wsTb_l = [sb("wsTb%d" % l, [P, 4, P], BF16) for l in range(self.depth)]
        self.wbaf = sb("wbaf", [P, 8, 8], F32)
        self.wbab_l = [sb("wbab%d" % l, [P, 8, 8], BF16) for l in range(self.depth)]
        self.ealog_l = [sb("ealog%d" % l, [P, 4], F32) for l in range(self.depth)]
        self.psum = [self.es.enter_context(self.nc.psum_tensor("psb%d" % i, [P, 512], F32)) for i in range(8)]

    def set_layer(self, l):
        self.cur_l = l
        self.pcol, self.wsTb, self.wbab = self.pcol_l[l], self.wsTb_l[l], self.wbab_l[l]
        self.S, self.haloq, self.halof, self.ealog = self.S_l[l], self.haloq_l[l], self.halof_l[l], self.ealog_l[l]
        self.Sb = self.Sb_l[l]
        self.wblk, self.wba, self.pcol_d, self.prow_d, self.wsT_d = (self.wblk_l[l], self.wba_l[l], self.pcol_dl[l],
                                                                       self.prow_dl[l], self.wsT_dl[l])
        for k in ("pcol", "wsTb", "wbab", "S", "Sb", "ealog"):
            self.s.alias[k] = "%s@%d" % (k, l)

    def prologue(self):
        stg = self.y[:, 0:2, :].rearrange("p a b -> p (a b)")
        self.dma("sp", stg, self.cst_d[:, :], (), ["y0", "y1"], "d_cst")
        stg3 = stg.rearrange("p (a b) -> p a b", b=P)
        self.cp("dve", self.cstb[:], stg3, ["y0", "y1"], ["cstb"])
        self.cp("dve", self.cst[:], stg3[:, 0:3, :], ["y0", "y1"], ["cst"])
        self.spu = stg3[:, C_SPU:C_SPU + 1, :]
        self.memset("pool", self.epsc[:, 0:1], LN_EPS, ["epsc"])
        self.memset("pool", self.epsc[:, 1:2], RMS_EPS, ["epsc"])
        self.memset("pool", self.epsc[:, 2:3], L2_EPS, ["epsc"])
        if self.pp:
            self.dma("sp", self.flg[:], self.flag_d[:, :], (), ["flg"], "d_flg")
        for l in range(self.depth):
            self.set_layer(l)
            self.prologue_layer()

    def prologue_layer(self):
        self.dma("sp", self.pcol[:], self.pcol_d[:, :], (), ["pcol"], "d_pcol%d" % self.cur_l)
        wst = self.y[:, 2, :]
        self.dma("sp", wst, self.wsT_d[:, :], (), ["y2"], "d_wsT")
        self.dma("sp", self.wbaf[:].rearrange("p a b -> p (a b)"), self.wba[:, :], (), ["wbaf"], "d_wba")
        self.cp("dve", self.wbab[:], self.wbaf[:], ["wbaf"], ["wbab"])
        self.tt("dve", self.wsTb[:], wst.rearrange("p (a b) -> p a b", b=P), self.spu.to_broadcast([P, 4, P]),
                ALU.mult, ["y2", "y0", "y1"], ["wsTb"])
        self.memset("pool", self.haloq[:], 0.0, ["haloq%d_%d" % (c, self.cur_l) for c in range(12)])
        self.memset("pool", self.halof[:], 0.0, ["halof%d_%d" % (c, self.cur_l) for c in range(44)])
        self.memset("pool", self.S[:], 0.0, ["S"])
        self.memset("pool", self.Sb[:], 0.0, ["Sb"])
        self.act(self.ealog[:], self.pcol[:, ALOG:ALOG + 4], AF.Exp, ["pcol"], ["ealog"])

    def wload(self, blk):
        slot = self.wn % self.NW
        self.wn += 1
        t = self.wring[slot]
        self.dma("pool", t[:].rearrange("p a b -> p (a b)"), self.wblk[blk, :, :], (), ["wr%d" % slot], "d_w%d" % slot)
        return slot

    def step(self, s, l):
        self.set_layer(l)
        tok0 = s * T
        xT, y, actb = self.xT, self.y, self.actb
        pcol = self.pcol
        cst, cstb = self.cst, self.cstb
        ident = cst[:, C_ID, :]
        identb = cstb[:, C_ID, :]
        onesb = cstb[:, C_ONE, :]

        pending = []
        order = [0, 1, 2, 3, 4, 9, 5, 6, 7, 8] + list(range(10, NBLK))
        nxt = [0]

        def prefetch():
            while nxt[0] < NBLK and len(pending) < self.NW:
                pending.append((order[nxt[0]], self.wload(order[nxt[0]])))
                nxt[0] += 1

        def getw(blk):
            prefetch()
            b, slot = pending.pop(0)
            assert b == blk
            return slot

        prefetch()
        self.dma("sp", self.prow[:], self.prow_d[:, :], (), ["prow"], "d_prow")
        if l == 0 and (s == 0 or not self.pp):
            self.dma("sp", xT[:], self.xin[:, tok0:tok0 + T].rearrange("(c p) t -> p c t", p=P), (),
                     ["xT%d" % c for c in range(8)], "d_x")
        if self.pp and s >= 2:
            stg = self.qkv
            for c in range(8):
                self.stt("dve", xT[:, c, :], stg[:, c, :], self.flg[:, 0:1], xT[:, c, :], ALU.mult, ALU.add,
                         ["qkv%d" % c, "xT%d" % c, "flg"], ["xT%d" % c])
        for c in range(8):
            self.cp("act" if c % 2 == 0 else "pool", actb[:, c, :], xT[:, c, :], ["xT%d" % c], ["ab%d" % c])
        abk = ["ab%d" % c for c in range(8)]

        b = self.ps()
        pst = self.psum[b]
        for tb in range(4):
            for kc in range(8):
                self.mm(pst[:, tb * 8:tb * 8 + 8], actb[:, kc, tb * P:(tb + 1) * P], self.wbab[:, kc, :], kc == 0, kc == 7,
                        ["wbab", "ab%d" % kc], ["ps%d" % b])
        ba = self.ba
        self.cp("act", ba[:].rearrange("p a b -> p (a b)"), pst[:, 0:32], ["ps%d" % b], ["ba"])
        cols = self.cols
        sc4 = self.small[:, 10:12, :].rearrange("p a (b c) -> p (a b) c", c=4)
        sd4 = self.small[:, 12:14, :].rearrange("p a (b c) -> p (a b) c", c=4)
        self.act(sc4, ba[:, :, 0:4], AF.Softplus, ["ba"], ["sc"], scale=-1.0)
        self.tt("dve", sd4, ba[:, :, 4:8], pcol[:, DTB:DTB + 4].rearrange("p (o c) -> p o c", o=1).to_broadcast([P, 4, 4]),
                ALU.add, ["ba", "pcol"], ["sd"])
        self.act(sd4, sd4, AF.Softplus, ["sd"], ["sd"])
        self.act(cols[:, 0, :, :], sc4, AF.Exp, ["sc"], ["colb"], scale=-1.0)
        self.ts("dve", cols[:, 1, :, :], sc4, -1.0, None, ALU.mult, None, ["sc"], ["collb"])
        self.tt("dve", cols[:, 3, :, :], sd4, self.ealog[:].rearrange("p (o c) -> p o c", o=1).to_broadcast([P, 4, 4]),
                ALU.mult, ["sd", "ealog"], ["colng"])
        self.ts("dve", cols[:, 2, :, :], cols[:, 3, :, :], -1.0, None, ALU.mult, None, ["colng"], ["colg"])
        b2 = self.ps()
        p2 = self.psum[b2]
        for tb in range(4):
            self.mm(p2[:, tb * 8:tb * 8 + 4], cst[:, C_TRIU, :], cols[:, 2, tb, :], True, True, ["cst", "colg"], ["ps%d" % b2])
            self.mm(p2[:, tb * 8 + 4:tb * 8 + 8], cst[:, C_BLK, :], cols[:, 2, tb, :], True, True, ["cst", "colg"], ["ps%d" % b2])
        p28 = p2[:, 0:32].rearrange("p (a b) -> p a b", b=8)
        self.cp("dve", cols[:, 4, :, :], p28[:, :, 0:4], ["ps%d" % b2], ["colG"])
        self.tt("dve", sc4, p28[:, :, 0:4], cols[:, 1, :, :], ALU.add, ["ps%d" % b2, "collb"], ["sc"])
        self.tt("dve", sd4, p28[:, :, 4:8], cols[:, 4, :, :], ALU.subtract, ["ps%d" % b2, "colG"], ["sd"])
        self.act(cols[:, 6, :, :], sc4, AF.Exp, ["sc"], ["colk"])
        self.act(cols[:, 7, :, :], sd4, AF.Exp, ["sd"], ["colkd"])

        def fm_chunk(wt, slot, c, m):
            b = self.ps()
            pst = self.psum[b]
            for kc in range(8):
                self.mm(pst[:], wt[:, kc, m * P:(m + 1) * P], actb[:, kc, :], kc == 0, kc == 7,
                        ["wr%d" % slot, "ab%d" % kc], ["ps%d" % b])
            if c < 12:
                self.conv_evac(pst, b, c, 4, self.haloq, pcol[:, CQ + 4 * c:CQ + 4 * c + 4], "q")
                accn = self.lastacc
                self.act(self.qkv[:, c, :], self.acc[accn][:], AF.Silu, ["acc%d" % accn], ["qkv%d" % c])
            elif c < 16:
                self.act(self.zs[:, c - 12, :], pst[:], AF.Silu, ["ps%d" % b], ["zs%d" % (c - 12)])
            elif c < 20:
                self.gelu(pst[:], b, self.ug[:, c - 16, :], ["ug%d" % (c - 16)])
            else:
                self.act(self.gates[:, c - 20, :], pst[:], AF.Sigmoid, ["ps%d" % b], ["gt%d" % (c - 20)])

        def l2n(c):
            sq = self.tmpb[c % 2]
            kq = "tmpb%d" % (c % 2)
            self.act(sq[:], self.qkv[:, c, :], AF.Square, ["qkv%d" % c], [kq])
            b = self.ps()
            pst = self.psum[b]
            self.mm(pst[:], onesb, sq[:], True, True, ["cstb", kq], ["ps%d" % b])
            rn = self.tmp[2 + c % 2]
            kr = "tmp%d" % (2 + c % 2)
            self.rsqrt(rn[:], pst[:], 2, 1.0, ["ps%d" % b], [kr])
            sc_ = (128.0 ** -0.5) if c < 4 else 1.0
            self.stt("pool", self.qkv[:, c, :], self.qkv[:, c, :], sc_, rn[:], ALU.mult, ALU.mult, ["qkv%d" % c, kr], ["qkv%d" % c])

        for j in range(4):
            slot = getw(B_IN + j)
            wt = self.wring[slot]
            for m in range(4):
                fm_chunk(wt, slot, 4 * j + m, m)
                if j >= 2:
                    l2n(4 * (j - 2) + m)
            if j == 0 and self.deferred is not None:
                self.deferred()
                self.deferred = None

        def sgv_unit(wt, slot, tb):
            b = self.ps()
            pst = self.psum[b]
            for kc in range(8):
                self.mm(pst[:], actb[:, kc, tb * P:(tb + 1) * P], wt[:, kc, :], kc == 0, kc == 7,
                        ["wr%d" % slot, "ab%d" % kc], ["ps%d" % b])
            t0 = self.tmp[tb % 2]
            k0 = "tmp%d" % (tb % 2)
            self.gelu(pst[:], b, t0[:], [k0])
            st = self.small[:, 1 + tb, 0:6]
            self.s.add("dve", lambda e, st=st, t0=t0: e.bn_stats(out=st, in_=t0[:]), [k0], ["bnst%d" % tb])
            mv = self.small[:, 5 + tb, 0:2]
            self.s.add("dve", lambda e, st=st, mv=mv: e.bn_aggr(out=mv, in_=st), ["bnst%d" % tb], ["bnmv%d" % tb])
            rs = self.small[:, 5 + tb, 2:3]
            self.rsqrt(rs, self.small[:, 5 + tb, 1:2], 0, 1.0, ["bnmv%d" % tb], ["bnrs%d" % tb])
            self.ts("dve", t0[:], t0[:], self.small[:, 5 + tb, 0:1], rs, ALU.subtract, ALU.mult,
                    [k0, "bnmv%d" % tb, "bnrs%d" % tb], [k0])
            self.tt("pool", t0[:], t0[:], self.prow[:, 0:512], ALU.mult, [k0, "prow"], [k0])
            self.tt("pool", self.sgv[:, tb, :], t0[:], self.prow[:, 512:1024], ALU.add, [k0, "prow"], ["sgv%d" % tb])

        def filler_gen():
            slot = getw(B_IN + 4)
            for m in range(4):
                fm_chunk(self.wring[slot], slot, 16 + m, m)
                yield
            slot = getw(B_IN + 9)
            for tb_ in range(4):
                sgv_unit(self.wring[slot], slot, tb_)
                yield
            for j in range(5, 9):
                slot = getw(B_IN + j)
                for m in range(4):
                    fm_chunk(self.wring[slot], slot, 4 * j + m, m)
                    yield

        self.filler = filler_gen()
        self.fill_ctr = 0
        if self.pending_cc is not None:
            par = self.pending_cc
            self.pending_cc = None
            snd, gat = self.snd[par], self.gat[par]
            self.s.add("pool", lambda e, snd=snd, gat=gat: e.collective_compute(
                "AllGather", ALU.bypass, replica_groups=[[0, 1], [2, 3], [4, 5], [6, 7]],
                ins=[snd.ap().opt()], outs=[gat.ap().opt()]), ["snd%d" % par], ["gat%d" % par],
                dma=True, semkey="cc", inc=1)
        self.ps_n = 5
        self.psn = 0
        self.scan_gen = None
        for tb in range(4):
            self.deltanet_block(tb)
            if self.scan_gen is not None:
                for _ in self.scan_gen:
                    pass
            self.scan_gen = self.dn_scan(tb)
        for _ in self.scan_gen:
            pass
        self.scan_gen = None
        self.ps_n = 8
        if self.filler is not None:
            for _ in self.filler:
                pass
        self.filler = None

        for h in range(4):
            sq = self.tmpb[h % 2]
            kq = "tmpb%d" % (h % 2)
            self.act(sq[:], self.osb[:, h, :], AF.Square, ["osb%d" % h], [kq])
            b = self.ps()
            pst = self.psum[b]
            self.mm(pst[:], onesb, sq[:], True, True, ["cstb", kq], ["ps%d" % b])
            rn = self.tmp[2 + h % 2]
            kr = "tmp%d" % (2 + h % 2)
            self.rsqrt(rn[:], pst[:], 1, 1.0 / 128.0, ["ps%d" % b], [kr])
            self.stt("pool", rn[:], self.osb[:, h, :], pcol[:, NW:NW + 1], rn[:], ALU.mult, ALU.mult, ["osb%d" % h, kr, "pcol"], [kr])
            self.tt("pool", self.oab[:, h, :], rn[:], self.zs[:, h, :], ALU.mult, [kr, "zs%d" % h], ["oab%d" % h])

        for g in range(4):
            b = self.ps()
            pst = self.psum[b]
            for tb in range(4):
                self.mm(pst[:, tb * P:(tb + 1) * P], self.sgv[:, tb, g * P:(g + 1) * P], self.wsTb[:, g, :], True, True,
                        ["sgv%d" % tb, "wsTb"], ["ps%d" % b])
            t0 = self.tmp[g % 2]
            k0 = "tmp%d" % (g % 2)
            bias = self.prow[:, 1024 + g * P:1024 + (g + 1) * P]
            for tb in range(4):
                self.tt("dve", t0[:, tb * P:(tb + 1) * P], pst[:, tb * P:(tb + 1) * P], bias, ALU.add, ["ps%d" % b, "prow"], [k0])
            self.tt("pool", self.oab[:, 4 + g, :], t0[:], self.ug[:, g, :], ALU.mult, [k0, "ug%d" % g], ["oab%d" % (4 + g)])

        for j in range(2):
            slot = getw(B_AB + j)
            wt = self.wring[slot]
            for m in range(4):
                c = 4 * j + m
                ba_ = self.ps()
                bb_ = self.ps()
                pa, pb = self.psum[ba_], self.psum[bb_]
                for kc in range(4):
                    self.mm(pa[:], wt[:, kc, m * P:(m + 1) * P], self.oab[:, kc, :], kc == 0, kc == 3,
                            ["wr%d" % slot, "oab%d" % kc], ["ps%d" % ba_])
                for kc in range(4):
                    self.mm(pb[:], wt[:, 4 + kc, m * P:(m + 1) * P], self.oab[:, 4 + kc, :], kc == 0, kc == 3,
                            ["wr%d" % slot, "oab%d" % (4 + kc)], ["ps%d" % bb_])
                t0 = self.tmp[c % 2]
                k0 = "tmp%d" % (c % 2)
                t1 = self.tmp[2 + c % 2]
                k1 = "tmp%d" % (2 + c % 2)
                self.tt("dve", t0[:], pa[:], self.gates[:, c, :], ALU.mult, ["ps%d" % ba_, "gt%d" % c], [k0])
                self.tt("dve", t1[:], pb[:], self.gates[:, 8 + c, :], ALU.mult, ["ps%d" % bb_, "gt%d" % (8 + c)], [k1])
                self.tt("pool", actb[:, c, :], t0[:], t1[:], ALU.add, [k0, k1], ["ab%d" % c])

        self.proj_res_ln(B_OUT, getw, 8, actb, xT, "xT", y, "y", L1G, L1B, actb, "ab")
        if self.pp and s + 1 < self.n_steps:
            self.dma("sp", xT[:], self.xin[:, tok0 + T:tok0 + 2 * T].rearrange("(c p) t -> p c t", p=P), (),
                     ["xT%d" % c for c in range(8)], "d_x")

        for j in range(11):
            slot = getw(B_UP + j)
            wt = self.wring[slot]
            accs = []
            for m in range(4):
                ci = 4 * j + m
                b = self.ps()
                pst = self.psum[b]
                for kc in range(8):
                    self.mm(pst[:], wt[:, kc, m * P:(m + 1) * P], actb[:, kc, :], kc == 0, kc == 7,
                            ["wr%d" % slot, "ab%d" % kc], ["ps%d" % b])
                self.conv_evac(pst, b, ci, 3, self.halof, pcol[:, CF + 3 * ci:CF + 3 * ci + 3], "f")
                accs.append(self.lastacc)
                if m >= 2:
                    an = accs[m - 2]
                    bn = accs[m]
                    sa = self.tmp[m % 2]
                    ks = "tmp%d" % (m % 2)
                    self.act(sa[:], self.acc[an][:], AF.Silu, ["acc%d" % an], [ks])
                    hi = 2 * j + (m - 2)
                    self.tt("pool", self.hid[:, hi, :], sa[:], self.acc[bn][:], ALU.mult, [ks, "acc%d" % bn], ["qkv%d" % (hi // 2)])

        if self.pp:
            self.recv_next = (s + 1) if (2 <= s + 1 < self.n_steps) else None
            tail = self.proj_res_ln(B_DN, getw, 22, self.hid, y, "y", y, "y", L2G, L2B, None, None, defer=True)
            assert nxt[0] == NBLK and not pending
            yout, snd_l, n_steps_ = self.yout, self.snd, self.n_steps
            pcol_l = self.pcol

            def finish(s=s, tok0=tok0, tail=tail, l=l):
                cur = self.cur_l
                self.set_layer(l)
                tail()
                self.dma("sp", yout[:, tok0:tok0 + T].rearrange("(c p) t -> p c t", p=P), y[:],
                         ["y%d" % c for c in range(8)], ["yout"], "d_y")
                if s < n_steps_ - 2:
                    par = s % 2
                    self.dma("sp", snd_l[par].ap().rearrange("(c p) t -> p c t", p=P), y[:],
                             ["y%d" % c for c in range(8)], ["snd%d" % par], "d_snd")
                    self.pending_cc = par
                self.set_layer(cur)

            self.deferred = finish
            if s == self.n_steps - 1:
                self.deferred()
                self.deferred = None
        else:
            self.proj_res_ln(B_DN, getw, 22, self.hid, y, "y", xT, "xT", L2G, L2B, None, None)
            outbuf, outk_ = xT, "xT"
            assert nxt[0] == NBLK and not pending
            if l == self.depth - 1:
                self.dma("sp", self.yout[:, tok0:tok0 + T].rearrange("(c p) t -> p c t", p=P), outbuf[:],
                         ["%s%d" % (outk_, c) for c in range(8)], ["yout"], "d_y")
        if self.pp:
            if s < 2:
                kc_ = self.flg[:, 2 + s:3 + s]
                fl2 = lambda t: t[:].rearrange("p a b -> p (a b)")
                hq = ["haloq%d_%d" % (c, l) for c in range(12)]
                hf = ["halof%d_%d" % (c, l) for c in range(44)]
                self.ts("dve", fl2(self.haloq), fl2(self.haloq), kc_, None, ALU.mult, None, hq + ["flg"], hq)
                self.ts("dve", fl2(self.halof), fl2(self.halof), kc_, None, ALU.mult, None, hf + ["flg"], hf)
                self.ts("dve", fl2(self.S), fl2(self.S), kc_, None, ALU.mult, None, ["S", "flg"], ["S"])
                self.ts("dve", fl2(self.Sb), fl2(self.Sb), kc_, None, ALU.mult, None, ["Sb", "flg"], ["Sb"])

    def fill(self, every=2):
        sg = getattr(self, "scan_gen", None)
        if sg is not None:
            try:
                next(sg)
            except StopIteration:
                self.scan_gen = None
        if self.filler is None:
            return
        self.fill_ctr += 1
        if self.fill_ctr % every:
            return
        try:
            next(self.filler)
        except StopIteration:
            self.filler = None

    def gelu(self, src, b, out, wkeys):
        self.act(out, src, AF.Gelu_apprx_tanh, ["ps%d" % b], wkeys)

    def rsqrt(self, out, src, epsi, scale, r, w):
        self.act(out, src, AF.Sqrt, r + ["epsc"], w, bias=self.epsc[:, epsi:epsi + 1], scale=scale)
        self.s.add("dve", lambda e: e.reciprocal(out=out, in_=out), w, w)

    def conv_evac(self, pst, b, ci, K, halo, wcols, tag):
        n = self.rawn % 3
        self.rawn += 1
        raw = self.raw[n]
        kr = "raw%d" % n
        acc = self.acc[n]
        ka = "acc%d" % n
        hk = "halo%s%d_%d" % (tag, ci, self.cur_l)
        H = K - 1
        self.cp("act", raw[:, 4:4 + T], pst[:], ["ps%d" % b], [kr + "m"])
        self.cp("pool", raw[:, 4 - H:4], halo[:, ci, 4 - H:4], [hk], [kr + "h"])
        rk = [kr + "m", kr + "h"]
        self.act(acc[:], pst[:], AF.Copy, ["ps%d" % b, "pcol"], [ka], scale=wcols[:, K - 1:K])
        for jj in range(1, K - 1):
            sh = (K - 1) - jj
            self.stt("dve", acc[:], raw[:, 4 - sh:4 - sh + T], wcols[:, jj:jj + 1], acc[:], ALU.mult, ALU.add,
                     rk + ["pcol", ka], [ka])
        sh = K - 1
        self.ts("pool", raw[:, 4 - sh:4 - sh + T], raw[:, 4 - sh:4 - sh + T], wcols[:, 0:1], None, ALU.mult, None,
                rk + ["pcol"], rk) if False else None
        self.stt("dve", acc[:], raw[:, 4 - sh:4 - sh + T], wcols[:, 0:1], acc[:], ALU.mult, ALU.add,
                 rk + ["pcol", ka], [ka])
        self.cp("pool", halo[:, ci, 4 - H:4], raw[:, 4 + T - H:4 + T], rk, [hk])
        self.lastacc = n

    def proj_res_ln(self, blk0, getw, nkc, src, res, resk, out, outk, gcol, bcol, outb, outbk, defer=False):
        pcol = self.pcol
        cstb = self.cstb
        onesb = cstb[:, C_ONE, :]
        srck = "ab" if src is self.actb else "hid"
        nparts = (nkc + 7) // 8
        r = out
        for j in range(2):
            banks = [self.ps() for _ in range(4)]
            for part in range(nparts):
                slot = getw(blk0 + j * nparts + part)
                wt = self.wring[slot]
                k0 = part * 8
                k1 = min(nkc, k0 + 8)
                for m in range(4):
                    pst = self.psum[banks[m]]
                    for kc in range(k0, k1):
                        sk = ("ab%d" % kc) if srck == "ab" else ("qkv%d" % (kc // 2))
                        self.mm(pst[:], wt[:, kc - k0, m * P:(m + 1) * P], src[:, kc, :], kc == 0, kc == nkc - 1,
                                ["wr%d" % slot, sk], ["ps%d" % banks[m]])
            for m in range(4):
                c = 4 * j + m
                pst = self.psum[banks[m]]
                self.stt("dve", r[:, c, :], res[:, c, :], ALPHA, pst[:], ALU.mult, ALU.add,
                         ["%s%d" % (resk, c), "ps%d" % banks[m]], ["%s%d" % (outk, c)])
        if getattr(self, "recv_next", None) is not None and src is self.hid:
            sn = self.recv_next
            self.recv_next = None
            g = self.gat[sn % 2]
            self.dma("sp", self.qkv[:, 0:8, :], g.ap()[0:D, :].rearrange("(c p) t -> p c t", p=P), ["gat%d" % (sn % 2)],
                     ["qkv%d" % c for c in range(8)], "d_rcv")
        if defer:
            return lambda: self.ln_tail(r, out, outk, gcol, bcol, outb, outbk)
        self.ln_tail(r, out, outk, gcol, bcol, outb, outbk)

    def ln_tail(self, r, out, outk, gcol, bcol, outb, outbk):
        pcol = self.pcol
        onesb = self.cstb[:, C_ONE, :]
        bm = self.ps()
        bq = self.ps()
        pm, pq = self.psum[bm], self.psum[bq]
        for c in range(8):
            rb = self.lnb[:, c % 2, :]
            rq = self.lnb[:, 2 + c % 2, :]
            kb_, kq_ = "lnb%d" % (c % 2), "lnb%d" % (2 + c % 2)
            self.cp("pool", rb, r[:, c, :], ["%s%d" % (outk, c)], [kb_])
            self.act(rq, r[:, c, :], AF.Square, ["%s%d" % (outk, c)], [kq_])
            self.mm(pm[:], onesb, rb, c == 0, c == 7, ["cstb", kb_], ["ps%d" % bm])
            self.mm(pq[:], onesb, rq, c == 0, c == 7, ["cstb", kq_], ["ps%d" % bq])
        st = self.stat
        self.ts("dve", st[:, 0, :], pm[:], 1.0 / D, None, ALU.mult, None, ["ps%d" % bm], ["osb0"])
        self.tt("dve", st[:, 2, :], st[:, 0, :], st[:, 0, :], ALU.mult, ["osb0"], ["osb2"])
        self.stt("dve", st[:, 1, :], pq[:], 1.0 / D, st[:, 2, :], ALU.mult, ALU.subtract, ["ps%d" % bq, "osb2"], ["osb1"])
        self.rsqrt(st[:, 1, :], st[:, 1, :], 0, 1.0, ["osb1"], ["osb1"])
        self.stt("dve", st[:, 2, :], st[:, 0, :], -1.0, st[:, 1, :], ALU.mult, ALU.mult, ["osb0", "osb1"], ["osb2"])
        for c in range(8):
            t0 = self.tmp[c % 4]
            k0 = "tmp%d" % (c % 4)
            kk = "%s%d" % (outk, c)
            self.tt("dve", t0[:], r[:, c, :], st[:, 1, :], ALU.mult, [kk, "osb1"], [k0])
            self.tt("dve" if c % 2 == 0 else "pool", t0[:], t0[:], st[:, 2, :], ALU.add, [k0, "osb2"], [k0])
            gc_ = pcol[:, gcol + c:gcol + c + 1]
            bc_ = pcol[:, bcol + c:bcol + c + 1]
            self.act(out[:, c, :], t0[:], AF.Identity, [k0, "pcol"], [kk], bias=bc_, scale=gc_)
            if outb is not None:
                self.act(outb[:, c, :], t0[:], AF.Identity, [k0, "pcol"], ["%s%d" % (outbk, c)], bias=bc_, scale=gc_)

    def deltanet_block(self, tb):
        cst, cstb = self.cst, self.cstb
        ident = cst[:, C_ID, :]
        identb = cstb[:, C_ID, :]
        triu = cst[:, C_TRIU, :]
        cols = self.cols
        dn = self.dn
        qkv = self.qkv
        blk = slice(tb * P, (tb + 1) * P)
        kcols = ["colb", "collb", "colg", "colng", "colG", "colk", "colkd"]
        pr = tb % 2
        d2 = self.dn2[pr]

        def bc(kind, h):
            return cols[:, kind, tb, h:h + 1].to_broadcast([P, P])

        bU, bA, bL, bG = self.ps(), self.ps(), self.ps(), self.ps()
        pU, pA, pL, pG = self.psum[bU], self.psum[bA], self.psum[bL], self.psum[bG]
        for h in range(4):
            hs = slice(h * P, (h + 1) * P)
            self.mm(pU[:, hs], bc(2, h), triu, True, False, kcols + ["cst"], ["ps%d" % bU])
            self.mm(pU[:, hs], triu, bc(3, h), False, False, kcols + ["cst"], ["ps%d" % bU])
            self.mm(pU[:, hs], bc(1, h), ident, False, False, kcols + ["cst"], ["ps%d" % bU])
            self.mm(pU[:, hs], identb, cstb[:, C_MUS, :], False, True, ["cstb"], ["ps%d" % bU])
            self.mm(pA[:, hs], bc(2, h), triu, True, False, kcols + ["cst"], ["ps%d" % bA])
            self.mm(pA[:, hs], triu, bc(3, h), False, False, kcols + ["cst"], ["ps%d" % bA])
            self.mm(pA[:, hs], identb, cstb[:, C_MUI, :], False, True, ["cstb"], ["ps%d" % bA])
            self.mm(pL[:, hs], triu, bc(2, h), True, False, kcols + ["cst"], ["ps%d" % bL])
            self.mm(pL[:, hs], bc(3, h), triu, False, False, kcols + ["cst"], ["ps%d" % bL])
            self.mm(pL[:, hs], ident, bc(1, h), False, False, kcols + ["cst"], ["ps%d" % bL])
            self.mm(pL[:, hs], identb, cstb[:, C_NLS, :], False, True, ["cstb"], ["ps%d" % bL])
            self.mm(pG[:, hs], bc(2, h), triu, True, True, kcols + ["cst"], ["ps%d" % bG])

        def fl(t):
            return t[:].rearrange("p a b -> p (a b)")

        self.act(fl(dn["DUs"]), pU[:], AF.Exp, ["ps%d" % bU], ["DUs"])
        self.act(fl(dn["DUi"]), pA[:], AF.Exp, ["ps%d" % bA], ["DUi"])
        self.act(fl(dn["DLs"]), pL[:], AF.Exp, ["ps%d" % bL], ["DLs"])
        self.act(fl(dn["EG"]), pG[:], AF.Exp, ["ps%d" % bG], ["EG"])

        qkb = self.qkb
        self.cp("act", qkb[:], qkv[:, 0:8, blk], ["qkv%d" % i for i in range(8)], ["qkb"])
        bK, bQ = self.ps(), self.ps()
        pK, pQ = self.psum[bK], self.psum[bQ]
        for h in range(4):
            hs = slice(h * P, (h + 1) * P)
            self.mm(pK[:, hs], qkb[:, 4 + h, :], qkb[:, 4 + h, :], True, True, ["qkb"], ["ps%d" % bK])
            self.mm(pQ[:, hs], qkb[:, 4 + h, :], qkb[:, h, :], True, True, ["qkb"], ["ps%d" % bQ])
        self.fill(1)
        self.stt("dve", fl(dn["Y"]), pK[:], -1.0, fl(dn["DUs"]), ALU.mult, ALU.mult, ["ps%d" % bK, "DUs"], ["Y"])
        self.stt("dve", fl(dn["YT"]), pK[:], -1.0, fl(dn["DLs"]), ALU.mult, ALU.mult, ["ps%d" % bK, "DLs"], ["YT"])
        self.tt("dve", fl(d2["Aqk"]), pQ[:], fl(dn["DUi"]), ALU.mult, ["ps%d" % bQ, "DUi"], ["Aqk%d" % pr])
        self.tt("pool", dn["Pm"][:], dn["Y"][:], cstb[:, C_ID:C_ID + 1, :].to_broadcast([P, 4, P]), ALU.add, ["Y", "cstb"], ["Pm"])
        self.tt("pool", d2["qg"][:], qkv[:, 0:4, blk], dn["EG"][:], ALU.mult, ["qkv0", "qkv1", "qkv2", "qkv3", "EG"], ["qg%d" % pr])
        self.cp("pool", d2["gl"][:, :, 0:1], dn["EG"][:, :, 63:64], ["EG"], ["gl%d" % pr])
        self.cp("pool", d2["gl"][:, :, 1:2], dn["EG"][:, :, 127:128], ["EG"], ["gl%d" % pr])

        bT, bV = self.ps(), self.ps()
        pT, pV = self.psum[bT], self.psum[bV]
        for h in range(4):
            hs = slice(h * P, (h + 1) * P)
            self.tr(pT[:, hs], qkv[:, 4 + h, blk], ident, ["qkv%d" % (4 + h), "cst"], ["ps%d" % bT])
            self.tr(pV[:, hs], qkv[:, 8 + h, blk], ident, ["qkv%d" % (8 + h), "cst"], ["ps%d" % bV])
        self.fill(1)

        def cb(kind):
            return cols[:, kind, tb, :].rearrange("p (h o) -> p h o", o=1).to_broadcast([P, 4, P])

        p3 = lambda ap: ap.rearrange("p (a b) -> p a b", a=4)
        self.tt("dve", dn["Kbg"][:], p3(pT[:]), cb(6), ALU.mult, ["ps%d" % bT] + kcols, ["Kbg"])
        self.tt("dve", d2["kd"][:], p3(pT[:]), cb(7), ALU.mult, ["ps%d" % bT] + kcols, ["kd%d" % pr])
        self.tt("dve", dn["Vb"][:], p3(pV[:]), cb(0), ALU.mult, ["ps%d" % bV] + kcols, ["Vb"])

        Z, ZT, Z2, ZT2 = "Y", "YT", "Z", "ZT"
        for lev in range(1, 6):
            last = lev == 5
            bz, bzt = self.ps(), self.ps()
            pz, pzt = self.psum[bz], self.psum[bzt]
            for h in range(4):
                hs = slice(h * P, (h + 1) * P)
                if not last:
                    self.mm(pz[:, hs], dn[ZT][:, h, :], dn[Z][:, h, :], True, True, [Z, ZT], ["ps%d" % bz])
                self.mm(pzt[:, hs], dn[Z][:, h, :], dn[ZT][:, h, :], True, True, [Z, ZT], ["ps%d" % bzt])
            self.fill(2)
            if not last:
                self.cp("act", fl(dn[Z2]), pz[:], ["ps%d" % bz], [Z2])
            self.cp("act", fl(dn[ZT2]), pzt[:], ["ps%d" % bzt], [ZT2])
            bp = self.ps()
            pp = self.psum[bp]
            for h in range(4):
                hs = slice(h * P, (h + 1) * P)
                self.mm(pp[:, hs], dn[ZT2][:, h, :], dn["Pm"][:, h, :], True, True, [ZT2, "Pm"], ["ps%d" % bp])
            self.fill(2)
            self.tt("dve", fl(dn["Pm"]), fl(dn["Pm"]), pp[:], ALU.add, ["Pm", "ps%d" % bp], ["Pm"])
            if Z == "Y":
                Z, ZT, Z2, ZT2 = "Z", "ZT", "Z2", "ZT2"
            else:
                Z, ZT, Z2, ZT2 = Z2, ZT2, Z, ZT

        bW, bUu = self.ps(), self.ps()
        pW, pUu = self.psum[bW], self.psum[bUu]
        for h in range(4):
            hs = slice(h * P, (h + 1) * P)
            self.mm(pW[:, hs], dn["Kbg"][:, h, :], dn["Pm"][:, h, :], True, True, ["Kbg", "Pm"], ["ps%d" % bW])
            self.mm(pUu[:, hs], dn["Pm"][:, h, :], dn["Vb"][:, h, :], True, True, ["Pm", "Vb"], ["ps%d" % bUu])
        self.fill(1)
        self.cp("act", fl(d2["WT"]), pW[:], ["ps%d" % bW], ["WT%d" % pr])
        self.cp("act", fl(d2["U"]), pUu[:], ["ps%d" % bUu], ["U%d" % pr])

    def dn_scan(self, tb):
        pr = tb % 2
        d2 = self.dn2[pr]
        blk = slice(tb * P, (tb + 1) * P)
        fl = lambda t: t[:].rearrange("p a b -> p (a b)")
        p3 = lambda ap: ap.rearrange("p (a b) -> p a b", a=4)
        bO, bws, bs = 5, 6, 7
        pO, pws, pS = self.psum[bO], self.psum[bws], self.psum[bs]
        S = self.S
        Sb = self.Sb
        for cc in range(2):
            rows = slice(cc * 64, (cc + 1) * 64)
            for h in range(4):
                hs = slice(h * P, (h + 1) * P)
                self.mm(pws[:, hs], d2["WT"][:, h, :], Sb[:, h, :], True, True, ["WT%d" % pr, "Sb"], ["ps%d" % bws])
            yield
            self.tt("dve", d2["un"][rows, :, :], d2["U"][rows, :, :], p3(pws[rows, :]), ALU.subtract,
                    ["U%d" % pr, "ps%d" % bws], ["un%d" % pr])
            yield
            for h in range(4):
                oc = slice(h * P + cc * 64, h * P + (cc + 1) * 64)
                self.mm(pO[:, oc], Sb[:, h, :], d2["qg"][:, h, rows], True, False, ["Sb", "qg%d" % pr], ["ps%d" % bO])
                self.mm(pO[:, oc], d2["un"][rows, h, :], d2["Aqk"][rows, h, rows], False, True,
                        ["un%d" % pr, "Aqk%d" % pr], ["ps%d" % bO])
                hs = slice(h * P, (h + 1) * P)
                self.mm(pS[:, hs], d2["kd"][rows, h, :], d2["un"][rows, h, :], True, True,
                        ["kd%d" % pr, "un%d" % pr], ["ps%d" % bs])
            glb = d2["gl"][:, :, cc:cc + 1].to_broadcast([P, 4, P])
            self.tt("pool", S[:], S[:], glb, ALU.mult, ["S", "gl%d" % pr], ["S"])
            yield
            self.tt("dve", fl(S), fl(S), pS[:], ALU.add, ["S", "ps%d" % bs], ["S"])
            self.cp("act", fl(Sb), fl(S), ["S"], ["Sb"])
            yield
        self.cp("act", self.osb[:, :, blk], p3(pO[:]), ["ps%d" % bO], ["osb0", "osb1", "osb2", "osb3"])


def _consts():
    idx = np.arange(P)
    same = (idx[:, None] // 64) == (idx[None, :] // 64)
    c = np.zeros((P, 8, P), np.float32)
    c[:, C_ID, :] = np.eye(P, dtype=np.float32)
    c[:, C_TRIU, :] = ((idx[:, None] <= idx[None, :]) & same)
    c[:, C_BLK, :] = same
    c[:, C_MUS, :] = np.where((idx[None, :] > idx[:, None]) & same, 0.0, -BIG)
    c[:, C_MUI, :] = np.where((idx[None, :] >= idx[:, None]) & same, 0.0, -BIG)
    c[:, C_NLS, :] = np.where((idx[None, :] < idx[:, None]) & same, 0.0, -BIG)
    c[:, C_SPU, :] = (idx[:, None] <= idx[None, :])
    c[:, C_ONE, :] = 1.0
    return np.ascontiguousarray(c.reshape(P, 8 * P))


def _blockify(w):
    K = w.shape[0]
    out = np.zeros((P, 8, 512), np.float32)
    nk = K // P
    out[:, :nk, :] = w.reshape(nk, P, 512).transpose(1, 0, 2)
    return out


def prep_layer(inp, l):
    w_in = np.asarray(inp["w_in"][l])
    qkvz = w_in[:, 0:2048]
    ba = w_in[:, 2048:2056]
    u = w_in[:, 2056:2568]
    sv = w_in[:, 2568:3080]
    gates = w_in[:, 3080:5128]
    fm = np.concatenate([qkvz, u, gates], axis=1)
    blocks = [_blockify(fm[:, j * 512:(j + 1) * 512]) for j in range(9)]
    blocks.append(_blockify(sv))
    wa = np.asarray(inp["w_branch_a"][l])
    wb = np.asarray(inp["w_branch_b"][l])
    for j in range(2):
        blocks.append(_blockify(np.concatenate([wa[:, j * 512:(j + 1) * 512], wb[:, j * 512:(j + 1) * 512]], axis=0)))
    wo = np.asarray(inp["w_out"][l])
    for j in range(2):
        blocks.append(_blockify(wo[:, j * 512:(j + 1) * 512]))
    wu = np.asarray(inp["w_up"][l])
    for j in range(11):
        blocks.append(_blockify(np.concatenate([wu[:, j * 256:(j + 1) * 256], wu[:, FFN + j * 256:FFN + (j + 1) * 256]], axis=1)))
    wd = np.asarray(inp["w_down"][l])
    for j in range(2):
        for part in range(3):
            k0 = part * 1024
            k1 = min(FFN, k0 + 1024)
            blocks.append(_blockify(wd[k0:k1, j * 512:(j + 1) * 512]))
    wblk = np.stack(blocks).reshape(NBLK, P, 4096)
    wba = np.ascontiguousarray(ba.reshape(8, P, 8).transpose(1, 0, 2).reshape(P, 64))
    pcol = np.zeros((P, NPC), np.float32)
    cq = np.asarray(inp["conv_qkv"][l])
    pcol[:, CQ:CQ + 48] = cq.reshape(4, 12, P).transpose(2, 1, 0).reshape(P, 48)
    cf = np.asarray(inp["conv_ffn"][l])
    cfa = cf[:, :FFN].reshape(3, 22, P)
    cfb = cf[:, FFN:].reshape(3, 22, P)
    cfo = np.zeros((P, 44, 3), np.float32)
    for j in range(11):
        for m in range(2):
            cfo[:, 4 * j + m, :] = cfa[:, 2 * j + m, :].T
            cfo[:, 4 * j + 2 + m, :] = cfb[:, 2 * j + m, :].T
    pcol[:, CF:CF + 132] = cfo.reshape(P, 132)
    pcol[:, NW] = np.asarray(inp["dn_norm_w"][l])
    pcol[:, L1G:L1G + 8] = np.asarray(inp["ln1_g"][l]).reshape(8, P).T
    pcol[:, L1B:L1B + 8] = np.asarray(inp["ln1_b"][l]).reshape(8, P).T
    pcol[:, L2G:L2G + 8] = np.asarray(inp["ln2_g"][l]).reshape(8, P).T
    pcol[:, L2B:L2B + 8] = np.asarray(inp["ln2_b"][l]).reshape(8, P).T
    pcol[:, ALOG:ALOG + 4] = np.asarray(inp["a_log"][l])[None, :]
    pcol[:, DTB:DTB + 4] = np.asarray(inp["dt_bias"][l])[None, :]
    prow = np.zeros((P, 1536), np.float32)
    prow[:, 0:512] = np.asarray(inp["sg_ln_g"][l])[None, :]
    prow[:, 512:1024] = np.asarray(inp["sg_ln_b"][l])[None, :]
    prow[:, 1024:1536] = np.asarray(inp["b_spatial"][l]).reshape(1, 512)
    ws = np.asarray(inp["w_spatial"][l])
    wsT = np.ascontiguousarray(ws.transpose(2, 0, 1).reshape(P, 512))
    return {"wblk": np.ascontiguousarray(wblk), "wba": wba, "pcol": pcol, "prow": prow, "wsT": wsT, "cst": _consts()}


_NC_CACHE = {}


def get_nc(n_steps, depth, pp=False):
    if (n_steps, depth, pp) not in _NC_CACHE:
        _NC_CACHE[(n_steps, depth, pp)] = Builder(n_steps, depth, pp).build()
    return _NC_CACHE[(n_steps, depth, pp)]


def run_pp(xT_list, lays, n_tiles):
    n_steps = n_tiles + 2
    nc = get_nc(n_steps, 1, True)
    in_maps = []
    for b, xT in enumerate(xT_list):
        for role in range(2):
            lay = lays[role]
            m = {"cst": lay["cst"]}
            for k in ("wblk", "wba", "pcol", "prow", "wsT"):
                m["%s0" % k] = lay[k]
            xin = np.zeros((D, n_steps * T), np.float32)
            flag = np.ones((P, 2 + n_steps), np.float32)
            if role == 0:
                xin[:, :n_tiles * T] = xT
                flag[:, 0] = 0.0
            else:
                flag[:, 2:4] = 0.0
            m["xin"] = xin
            m["flag"] = flag
            in_maps.append(m)
    res = run_bass_kernel_spmd(nc, in_maps, core_ids=list(range(len(in_maps))))
    return [res.results[2 * b + 1]["yout"][:, 2 * T:] for b in range(len(xT_list))]


def run_layers(xT_list, lays, n_steps):
    nc = get_nc(n_steps, len(lays))
    base = {"cst": lays[0]["cst"]}
    for l, lay in enumerate(lays):
        for k in ("wblk", "wba", "pcol", "prow", "wsT"):
            base["%s%d" % (k, l)] = lay[k]
    in_maps = []
    for xT in xT_list:
        m = dict(base)
        m["xin"] = np.ascontiguousarray(xT, dtype=np.float32)
        in_maps.append(m)
    res = run_bass_kernel_spmd(nc, in_maps, core_ids=list(range(len(xT_list))))
    return [r["yout"] for r in res.results]


def kernel(**inputs):
    x = np.asarray(inputs["x"], dtype=np.float32)
    n_steps = SEQ // T
    cur = [np.ascontiguousarray(x[b].T) for b in range(BATCH)]
    lays = [prep_layer(inputs, l) for l in range(DEPTH)]
    cur = run_pp(cur, lays, n_steps)
    out = np.stack([c.T for c in cur]).astype(np.float32)
    return out
```

```python
import numpy as np
from contextlib import ExitStack
import concourse.bass as bass
import concourse.mybir as mybir
from concourse.bass_utils import run_bass_kernel_spmd

F32 = mybir.dt.float32
BF16 = mybir.dt.bfloat16
AF = mybir.ActivationFunctionType
ALU = mybir.AluOpType

P = 128
T = 512
D = 1024
KC = 8
DEPTH = 2
SEQ = 8192
BATCH = 4
FFN = 2816
NBLK = 31
B_IN, B_AB, B_OUT, B_UP, B_DN = 0, 10, 12, 14, 25
ALPHA = (2 * DEPTH) ** 0.25
LN_EPS = 1e-5
RMS_EPS = 1e-6
L2_EPS = 1e-6
BIG = 32768.0
CQ, CF, NW, L1G, L1B, L2G, L2B, ALOG, DTB, NPC = 0, 48, 180, 181, 189, 197, 205, 213, 217, 221
C_ID, C_TRIU, C_BLK, C_MUS, C_MUI, C_NLS, C_SPU, C_ONE = range(8)
GELU_C = 1.5957691216057308


class _Op:
    __slots__ = ("eng", "fn", "deps", "signal", "tick", "semkey", "isdma", "raw", "inc")

    def __init__(self, eng, fn, isdma, semkey, inc=None):
        self.inc = inc if inc is not None else (16 if isdma else 1)
        self.eng = eng
        self.fn = fn
        self.deps = []
        self.signal = False
        self.tick = 0
        self.semkey = semkey
        self.isdma = isdma


class Sched:
    ENGS = ("pe", "act", "dve", "pool", "sp")

    def __init__(self):
        self.ops = {e: [] for e in self.ENGS}
        self.last_w = {}
        self.readers = {}
        self.all = []
        self.alias = {}

    def add(self, eng, fn, reads=(), writes=(), dma=False, semkey=None, inc=None):
        op = _Op(eng, fn, dma, semkey if dma else eng, inc)
        reads = [self.alias.get(k, k) for k in reads]
        writes = [self.alias.get(k, k) for k in writes]
        deps = {}
        for k in reads:
            w = self.last_w.get(k)
            if w is not None:
                deps[id(w)] = (w, True)
        for k in writes:
            w = self.last_w.get(k)
            if w is not None and id(w) not in deps:
                deps[id(w)] = (w, False)
            for r in self.readers.get(k, ()):
                if id(r) not in deps:
                    deps[id(r)] = (r, False)
        for d, israw in deps.values():
            if d.isdma:
                op.deps.append(d)
            elif d.eng == eng and not dma:
                if eng != "pe" and israw:
                    op.deps.append(d)
            else:
                op.deps.append(d)
        for d in op.deps:
            d.signal = True
        for k in reads:
            self.readers.setdefault(k, []).append(op)
        for k in writes:
            self.last_w[k] = op
            self.readers[k] = []
        self.ops[eng].append(op)
        self.all.append(op)
        return op

    def emit(self, nc, es):
        counts = {}
        for op in self.all:
            if op.signal:
                counts[op.semkey] = counts.get(op.semkey, 0) + op.inc
                op.tick = counts[op.semkey]
        sems = {}
        for k in counts:
            sems[k] = es.enter_context(nc.semaphore("s_" + str(k)))
        block = es.enter_context(nc.Block())

        def run(engname):
            def body(e):
                waited = {}
                for op in self.ops[engname]:
                    need = {}
                    for d in op.deps:
                        if need.get(d.semkey, 0) < d.tick:
                            need[d.semkey] = d.tick
                    for k, v in need.items():
                        if waited.get(k, 0) < v:
                            e.wait_ge(sems[k], v)
                            waited[k] = v
                    ins = op.fn(e)
                    if op.signal:
                        if op.isdma and op.inc == 1:
                            ins.then_inc(sems[op.semkey])
                        else:
                            ins.then_inc(sems[op.semkey], op.inc)
            return body

        block.tensor(run("pe"))
        block.scalar(run("act"))
        block.vector(run("dve"))
        block.gpsimd(run("pool"))
        block.sync(run("sp"))


class Builder:
    def __init__(self, n_steps, depth=1, pp=False):
        self.n_steps = n_steps
        self.depth = depth
        self.pp = pp
        self.pending_cc = None
        self.nc = bass.Bass("TRN2", target_bir_lowering=False)
        self.s = Sched()
        self.psn = 0
        self.wn = 0
        self.rawn = 0

    def mm(self, out, lhsT, rhs, start, stop, r, w):
        self.s.add("pe", lambda e: e.matmul(out, lhsT=lhsT, rhs=rhs, start=start, stop=stop), r, w)

    def tr(self, out, in_, ident, r, w):
        self.s.add("pe", lambda e: e.transpose(out, in_, ident), r, w)

    def act(self, out, in_, func, r, w, bias=None, scale=None):
        kw = {}
        if bias is not None:
            kw["bias"] = bias
        if scale is not None:
            kw["scale"] = scale
        self.s.add("act", lambda e: e.activation(out=out, in_=in_, func=func, **kw), r, w)

    def tt(self, eng, out, in0, in1, alu, r, w):
        self.s.add(eng, lambda e: e.tensor_tensor(out=out, in0=in0, in1=in1, op=alu), r, w)

    def ts(self, eng, out, in0, s1, s2, op0, op1, r, w):
        if op1 is None:
            self.s.add(eng, lambda e: e.tensor_scalar(out=out, in0=in0, scalar1=s1, scalar2=None, op0=op0), r, w)
        else:
            self.s.add(eng, lambda e: e.tensor_scalar(out=out, in0=in0, scalar1=s1, scalar2=s2, op0=op0, op1=op1), r, w)

    def stt(self, eng, out, in0, sc, in1, op0, op1, r, w):
        eng = "dve"
        self.s.add(eng, lambda e: e.scalar_tensor_tensor(out=out, in0=in0, scalar=sc, in1=in1, op0=op0, op1=op1), r, w)

    def cp(self, eng, out, in_, r, w):
        if eng == "act":
            self.s.add("act", lambda e: e.activation(out=out, in_=in_, func=AF.Copy), r, w)
        else:
            self.s.add(eng, lambda e: e.tensor_copy(out=out, in_=in_), r, w)

    def memset(self, eng, ap, val, w):
        self.s.add(eng, lambda e: e.memset(ap, val), (), w)

    def dma(self, eng, out, in_, r, w, semkey):
        self.s.add(eng, lambda e: e.dma_start(out=out, in_=in_), r, w, dma=True, semkey=semkey)

    def ps(self):
        n = getattr(self, "ps_n", 8)
        b = self.psn % n
        self.psn += 1
        return b

    def build(self):
        nc = self.nc
        n_steps = self.n_steps
        ntok = n_steps * T
        dt = nc.dram_tensor
        self.xin = dt("xin", [D, ntok], F32, kind="ExternalInput").ap()
        self.wblk_l = [dt("wblk%d" % l, [NBLK, P, 4096], F32, kind="ExternalInput").ap() for l in range(self.depth)]
        self.wba_l = [dt("wba%d" % l, [P, 64], F32, kind="ExternalInput").ap() for l in range(self.depth)]
        self.pcol_dl = [dt("pcol%d" % l, [P, NPC], F32, kind="ExternalInput").ap() for l in range(self.depth)]
        self.prow_dl = [dt("prow%d" % l, [P, 1536], F32, kind="ExternalInput").ap() for l in range(self.depth)]
        self.wsT_dl = [dt("wsT%d" % l, [P, 512], F32, kind="ExternalInput").ap() for l in range(self.depth)]
        self.cst_d = dt("cst", [P, 1024], F32, kind="ExternalInput").ap()
        self.yout = dt("yout", [D, ntok], F32, kind="ExternalOutput").ap()
        if self.pp:
            self.flag_d = dt("flag", [P, 2 + n_steps], F32, kind="ExternalInput").ap()
            self.snd = [nc.dram_tensor("snd%d" % i, [D, T], F32) for i in range(2)]
            self.gat = [nc.dram_tensor("gat%d" % i, [2 * D, T], F32) for i in range(2)]
        with ExitStack() as es:
            self.es = es
            self.alloc()
            self.prologue()
            for s in range(n_steps):
                for l in range(self.depth):
                    self.step(s, l)
            self.s.add("sp", lambda e: e.nop(), ["yout"], ())
            self.s.emit(nc, es)
        return nc

    def sb(self, name, shape, dtype):
        return self.es.enter_context(self.nc.sbuf_tensor(name, shape, dtype))

    def alloc(self):
        sb = self.sb
        self.xT = sb("xT", [P, 8, T], F32)
        self.y = sb("y", [P, 8, T], F32)
        self.actb = sb("actb", [P, 8, T], BF16)
        self.NW = 3
        self.wring = [sb("wr%d" % i, [P, 8, 512], BF16) for i in range(self.NW)]
        self.raw = [sb("raw%d" % i, [P, 516], F32) for i in range(3)]
        self.acc = [sb("acc%d" % i, [P, T], F32) for i in range(3)]
        self.qkv = sb("qkv", [P, 12, T], F32)
        self.zs = sb("zs", [P, 4, T], BF16)
        self.ug = sb("ug", [P, 4, T], BF16)
        self.gates = sb("gates", [P, 16, T], BF16)
        self.sgv = sb("sgv", [P, 4, T], BF16)
        self.osb = sb("osb", [P, 4, T], F32)
        self.hid = self.qkv[:].rearrange("p a b -> p (a b)").bitcast(BF16).rearrange("p (a b) -> p a b", b=T)
        self.oab = sb("oab", [P, 8, T], BF16)
        self.tmp = [sb("tmp%d" % i, [P, T], F32) for i in range(4)]
        self.lnb = sb("lnb", [P, 4, T], BF16)
        self.tmpb = [self.lnb[:, i, :] for i in range(2)]
        self.s.alias["tmpb0"] = "lnb0"
        self.s.alias["tmpb1"] = "lnb1"
        self.stat = self.osb
        self.haloq_l = [sb("haloq%d" % l, [P, 12, 4], F32) for l in range(self.depth)]
        self.halof_l = [sb("halof%d" % l, [P, 44, 4], F32) for l in range(self.depth)]
        self.S_l = [sb("S%d" % l, [P, 4, P], F32) for l in range(self.depth)]
        self.ba = sb("ba", [P, 4, 8], F32)
        self.cols = sb("cols", [P, 12, 4, 4], F32)
        self.dn = {n: sb("dn_" + n, [P, 4, P], F32) for n in ["DUs", "DUi", "DLs", "EG"]}
        for n in ["Y", "YT", "Pm", "Z", "ZT", "Z2", "ZT2", "Kbg", "Vb"]:
            self.dn[n] = sb("dn_" + n, [P, 4, P], BF16)
        self.qkb = sb("qkb", [P, 8, P], BF16)
        oav = self.oab[:].rearrange("p a b -> p (a b)").bitcast(F32).rearrange("p (k a b) -> p k a b", k=4, a=4)
        self.dec = [
            {n: self.dn[n][:] for n in ("DUs", "DUi", "DLs", "EG")},
            {n: oav[:, i, :, :] for i, n in enumerate(("DUs", "DUi", "DLs", "EG"))},
        ]
        self.deck = [
            {n: [n] for n in ("DUs", "DUi", "DLs", "EG")},
            {n: ["oab%d" % (2 * i), "oab%d" % (2 * i + 1)] for i, n in enumerate(("DUs", "DUi", "DLs", "EG"))},
        ]
        self.dn2 = []
        for i in range(2):
            d = {n: sb("dn%d_%s" % (i, n), [P, 4, P], BF16) for n in ["WT", "qg", "Aqk", "kd", "un"]}
            d["U"] = sb("dn%d_U" % i, [P, 4, P], F32)
            d["gl"] = sb("dn%d_gl" % i, [P, 4, 2], F32)
            self.dn2.append(d)
        self.Sb_l = [sb("Sb%d" % l, [P, 4, P], BF16) for l in range(self.depth)]
        self.small = sb("small", [P, 16, 8], F32)
        self.epsc = sb("epsc", [P, 4], F32)
        if self.pp:
            self.flg = sb("flg", [P, 2 + self.n_steps], F32)
        self.pcol_l = [sb("pcolsb%d" % l, [P, NPC], F32) for l in range(self.depth)]
        self.prow = sb("prowsb", [P, 1536], F32)
        self.cst = sb("cstsb", [P, 3, P], F32)
        self.cstb = sb("cstb", [P, 8, P], BF16)
        self.wsTb_l = [sb("wsTb%d" % l, [P, 4, P], BF16) for l in range(self.depth)]
        self.wbaf = sb("wbaf", [P, 8, 8], F32)
        self.wbab_l = [sb("wbab%d" % l, [P, 8, 8], BF16) for l in range(self.depth)]
        self.ealog_l = [sb("ealog%d" % l, [P, 4], F32) for l in range(self.depth)]
        self.psum = [self.es.enter_context(self.nc.psum_tensor("psb%d" % i, [P, 512], F32)) for i in range(8)]

    def set_layer(self, l):
        self.cur_l = l
        self.pcol, self.wsTb, self.wbab = self.pcol_l[l], self.wsTb_l[l], self.wbab_l[l]
        self.S, self.haloq, self.halof, self.ealog = self.S_l[l], self.haloq_l[l], self.halof_l[l], self.ealog_l[l]
        self.Sb = self.Sb_l[l]
        self.wblk, self.wba, self.pcol_d, self.prow_d, self.wsT_d = (self.wblk_l[l], self.wba_l[l], self.pcol_dl[l],
                                                                       self.prow_dl[l], self.wsT_dl[l])
        for k in ("pcol", "wsTb", "wbab", "S", "Sb", "ealog"):
            self.s.alias[k] = "%s@%d" % (k, l)

    def prologue(self):
        stg = self.y[:, 0:2, :].rearrange("p a b -> p (a b)")
        self.dma("sp", stg, self.cst_d[:, :], (), ["y0", "y1"], "d_cst")
        stg3 = stg.rearrange("p (a b) -> p a b", b=P)
        self.cp("dve", self.cstb[:], stg3, ["y0", "y1"], ["cstb"])
        self.cp("dve", self.cst[:], stg3[:, 0:3, :], ["y0", "y1"], ["cst"])
        self.spu = stg3[:, C_SPU:C_SPU + 1, :]
        self.memset("pool", self.epsc[:, 0:1], LN_EPS, ["epsc"])
        self.memset("pool", self.epsc[:, 1:2], RMS_EPS, ["epsc"])
        self.memset("pool", self.epsc[:, 2:3], L2_EPS, ["epsc"])
        if self.pp:
            self.dma("sp", self.flg[:], self.flag_d[:, :], (), ["flg"], "d_flg")
        for l in range(self.depth):
            self.set_layer(l)
            self.prologue_layer()

    def prologue_layer(self):
        self.dma("sp", self.pcol[:], self.pcol_d[:, :], (), ["pcol"], "d_pcol%d" % self.cur_l)
        wst = self.y[:, 2, :]
        self.dma("sp", wst, self.wsT_d[:, :], (), ["y2"], "d_wsT")
        self.dma("sp", self.wbaf[:].rearrange("p a b -> p (a b)"), self.wba[:, :], (), ["wbaf"], "d_wba")
        self.cp("dve", self.wbab[:], self.wbaf[:], ["wbaf"], ["wbab"])
        self.tt("dve", self.wsTb[:], wst.rearrange("p (a b) -> p a b", b=P), self.spu.to_broadcast([P, 4, P]),
                ALU.mult, ["y2", "y0", "y1"], ["wsTb"])
        self.memset("pool", self.haloq[:], 0.0, ["haloq%d_%d" % (c, self.cur_l) for c in range(12)])
        self.memset("pool", self.halof[:], 0.0, ["halof%d_%d" % (c, self.cur_l) for c in range(44)])
        self.memset("pool", self.S[:], 0.0, ["S"])
        self.memset("pool", self.Sb[:], 0.0, ["Sb"])
        self.act(self.ealog[:], self.pcol[:, ALOG:ALOG + 4], AF.Exp, ["pcol"], ["ealog"])

    def wload(self, blk):
        slot = self.wn % self.NW
        self.wn += 1
        t = self.wring[slot]
        self.dma("pool", t[:].rearrange("p a b -> p (a b)"), self.wblk[blk, :, :], (), ["wr%d" % slot], "d_w%d" % slot)
        return slot

    def step(self, s, l):
        self.set_layer(l)
        tok0 = s * T
        xT, y, actb = self.xT, self.y, self.actb
        pcol = self.pcol
        cst, cstb = self.cst, self.cstb
        ident = cst[:, C_ID, :]
        identb = cstb[:, C_ID, :]
        onesb = cstb[:, C_ONE, :]

        pending = []
        order = [0, 1, 2, 3, 4, 9, 5, 6, 7, 8] + list(range(10, NBLK))
        nxt = [0]

        def prefetch():
            while nxt[0] < NBLK and len(pending) < self.NW:
                pending.append((order[nxt[0]], self.wload(order[nxt[0]])))
                nxt[0] += 1

        def getw(blk):
            prefetch()
            b, slot = pending.pop(0)
            assert b == blk
            return slot

        prefetch()
        self.dma("sp", self.prow[:], self.prow_d[:, :], (), ["prow"], "d_prow")
        if l == 0 and (s == 0 or not self.pp):
            self.dma("sp", xT[:], self.xin[:, tok0:tok0 + T].rearrange("(c p) t -> p c t", p=P), (),
                     ["xT%d" % c for c in range(8)], "d_x")
        if self.pp and s >= 2:
            stg = self.qkv
            for c in range(8):
                self.stt("dve", xT[:, c, :], stg[:, c, :], self.flg[:, 0:1], xT[:, c, :], ALU.mult, ALU.add,
                         ["qkv%d" % c, "xT%d" % c, "flg"], ["xT%d" % c])
        for c in range(8):
            self.cp("act" if c % 2 == 0 else "pool", actb[:, c, :], xT[:, c, :], ["xT%d" % c], ["ab%d" % c])
        abk = ["ab%d" % c for c in range(8)]

        def fm_chunk(wt, slot, c, m):
            b = self.ps()
            pst = self.psum[b]
            for kc in range(8):
                self.mm(pst[:], wt[:, kc, m * P:(m + 1) * P], actb[:, kc, :], kc == 0, kc == 7,
                        ["wr%d" % slot, "ab%d" % kc], ["ps%d" % b])
            if c < 12:
                self.conv_evac(pst, b, c, 4, self.haloq, pcol[:, CQ + 4 * c:CQ + 4 * c + 4], "q")
                accn = self.lastacc
                self.act(self.qkv[:, c, :], self.acc[accn][:], AF.Silu, ["acc%d" % accn], ["qkv%d" % c])
            elif c < 16:
                self.act(self.zs[:, c - 12, :], pst[:], AF.Silu, ["ps%d" % b], ["zs%d" % (c - 12)])
            elif c < 20:
                self.gelu(pst[:], b, self.ug[:, c - 16, :], ["ug%d" % (c - 16)])
            else:
                self.act(self.gates[:, c - 20, :], pst[:], AF.Sigmoid, ["ps%d" % b], ["gt%d" % (c - 20)])

        for j in range(4):
            slot = getw(B_IN + j)
            wt = self.wring[slot]
            for m in range(4):
                fm_chunk(wt, slot, 4 * j + m, m)

        def sgv_unit(wt, slot, tb):
            b = self.ps()
            pst = self.psum[b]
            for kc in range(8):
                self.mm(pst[:], actb[:, kc, tb * P:(tb + 1) * P], wt[:, kc, :], kc == 0, kc == 7,
                        ["wr%d" % slot, "ab%d" % kc], ["ps%d" % b])
            t0 = self.tmp[tb % 2]
            k0 = "tmp%d" % (tb % 2)
            self.gelu(pst[:], b, t0[:], [k0])
            st = self.small[:, 1 + tb, 0:6]
            self.s.add("dve", lambda e, st=st, t0=t0: e.bn_stats(out=st, in_=t0[:]), [k0], ["bnst%d" % tb])
            mv = self.small[:, 5 + tb, 0:2]
            self.s.add("dve", lambda e, st=st, mv=mv: e.bn_aggr(out=mv, in_=st), ["bnst%d" % tb], ["bnmv%d" % tb])
            rs = self.small[:, 5 + tb, 2:3]
            self.rsqrt(rs, self.small[:, 5 + tb, 1:2], 0, 1.0, ["bnmv%d" % tb], ["bnrs%d" % tb])
            self.ts("dve", t0[:], t0[:], self.small[:, 5 + tb, 0:1], rs, ALU.subtract, ALU.mult,
                    [k0, "bnmv%d" % tb, "bnrs%d" % tb], [k0])
            self.tt("pool", t0[:], t0[:], self.prow[:, 0:512], ALU.mult, [k0, "prow"], [k0])
            self.tt("pool", self.sgv[:, tb, :], t0[:], self.prow[:, 512:1024], ALU.add, [k0, "prow"], ["sgv%d" % tb])

        def filler_gen():
            slot = getw(B_IN + 4)
            for m in range(4):
                fm_chunk(self.wring[slot], slot, 16 + m, m)
                yield
            slot = getw(B_IN + 9)
            for tb_ in range(4):
                sgv_unit(self.wring[slot], slot, tb_)
                yield
            for j in range(5, 9):
                slot = getw(B_IN + j)
                for m in range(4):
                    fm_chunk(self.wring[slot], slot, 4 * j + m, m)
                    yield

        self.filler = filler_gen()
        self.fill_ctr = 0
        b = self.ps()
        pst = self.psum[b]
        for tb in range(4):
            for kc in range(8):
                self.mm(pst[:, tb * 8:tb * 8 + 8], actb[:, kc, tb * P:(tb + 1) * P], self.wbab[:, kc, :], kc == 0, kc == 7,
                        ["wbab", "ab%d" % kc], ["ps%d" % b])
        ba = self.ba
        self.cp("act", ba[:].rearrange("p a b -> p (a b)"), pst[:, 0:32], ["ps%d" % b], ["ba"])
        cols = self.cols
        sc4 = self.small[:, 10:12, :].rearrange("p a (b c) -> p (a b) c", c=4)
        sd4 = self.small[:, 12:14, :].rearrange("p a (b c) -> p (a b) c", c=4)
        self.act(sc4, ba[:, :, 0:4], AF.Softplus, ["ba"], ["sc"], scale=-1.0)
        self.tt("dve", sd4, ba[:, :, 4:8], pcol[:, DTB:DTB + 4].rearrange("p (o c) -> p o c", o=1).to_broadcast([P, 4, 4]),
                ALU.add, ["ba", "pcol"], ["sd"])
        self.act(sd4, sd4, AF.Softplus, ["sd"], ["sd"])
        self.act(cols[:, 0, :, :], sc4, AF.Exp, ["sc"], ["colb"], scale=-1.0)
        self.ts("dve", cols[:, 1, :, :], sc4, -1.0, None, ALU.mult, None, ["sc"], ["collb"])
        self.tt("dve", cols[:, 3, :, :], sd4, self.ealog[:].rearrange("p (o c) -> p o c", o=1).to_broadcast([P, 4, 4]),
                ALU.mult, ["sd", "ealog"], ["colng"])
        self.ts("dve", cols[:, 2, :, :], cols[:, 3, :, :], -1.0, None, ALU.mult, None, ["colng"], ["colg"])
        b2 = self.ps()
        p2 = self.psum[b2]
        for tb in range(4):
            self.mm(p2[:, tb * 8:tb * 8 + 4], cst[:, C_TRIU, :], cols[:, 2, tb, :], True, True, ["cst", "colg"], ["ps%d" % b2])
            self.mm(p2[:, tb * 8 + 4:tb * 8 + 8], cst[:, C_BLK, :], cols[:, 2, tb, :], True, True, ["cst", "colg"], ["ps%d" % b2])
        p28 = p2[:, 0:32].rearrange("p (a b) -> p a b", b=8)
        self.cp("dve", cols[:, 4, :, :], p28[:, :, 0:4], ["ps%d" % b2], ["colG"])
        self.tt("dve", sc4, p28[:, :, 0:4], cols[:, 1, :, :], ALU.add, ["ps%d" % b2, "collb"], ["sc"])
        self.tt("dve", sd4, p28[:, :, 4:8], cols[:, 4, :, :], ALU.subtract, ["ps%d" % b2, "colG"], ["sd"])
        self.act(cols[:, 6, :, :], sc4, AF.Exp, ["sc"], ["colk"])
        self.act(cols[:, 7, :, :], sd4, AF.Exp, ["sd"], ["colkd"])

        for c in range(8):
            sq = self.tmpb[c % 2]
            kq = "tmpb%d" % (c % 2)
            self.act(sq[:], self.qkv[:, c, :], AF.Square, ["qkv%d" % c], [kq])
            b = self.ps()
            pst = self.psum[b]
            self.mm(pst[:], onesb, sq[:], True, True, ["cstb", kq], ["ps%d" % b])
            rn = self.tmp[2 + c % 2]
            kr = "tmp%d" % (2 + c % 2)
            self.rsqrt(rn[:], pst[:], 2, 1.0, ["ps%d" % b], [kr])
            sc_ = (128.0 ** -0.5) if c < 4 else 1.0
            self.stt("pool", self.qkv[:, c, :], self.qkv[:, c, :], sc_, rn[:], ALU.mult, ALU.mult, ["qkv%d" % c, kr], ["qkv%d" % c])

        if self.pending_cc is not None:
            par = self.pending_cc
            self.pending_cc = None
            snd, gat = self.snd[par], self.gat[par]
            self.s.add("pool", lambda e, snd=snd, gat=gat: e.collective_compute(
                "AllGather", ALU.bypass, replica_groups=[[0, 1], [2, 3], [4, 5], [6, 7]],
                ins=[snd.ap().opt()], outs=[gat.ap().opt()]), ["snd%d" % par], ["gat%d" % par],
                dma=True, semkey="cc", inc=1)
        self.ps_n = 5
        self.psn = 0
        self.scan_gen = None
        self.dn_masks(0)
        for tb in range(4):
            self.deltanet_block(tb)
            if self.scan_gen is not None:
                for _ in self.scan_gen:
                    pass
            self.scan_gen = self.dn_scan(tb)
        for _ in self.scan_gen:
            pass
        self.scan_gen = None
        self.ps_n = 8
        if self.filler is not None:
            for _ in self.filler:
                pass
        self.filler = None

        for h in range(4):
            sq = self.tmpb[h % 2]
            kq = "tmpb%d" % (h % 2)
            self.act(sq[:], self.osb[:, h, :], AF.Square, ["osb%d" % h], [kq])
            b = self.ps()
            pst = self.psum[b]
            self.mm(pst[:], onesb, sq[:], True, True, ["cstb", kq], ["ps%d" % b])
            rn = self.tmp[2 + h % 2]
            kr = "tmp%d" % (2 + h % 2)
            self.rsqrt(rn[:], pst[:], 1, 1.0 / 128.0, ["ps%d" % b], [kr])
            self.stt("pool", rn[:], self.osb[:, h, :], pcol[:, NW:NW + 1], rn[:], ALU.mult, ALU.mult, ["osb%d" % h, kr, "pcol"], [kr])
            self.tt("pool", self.oab[:, h, :], rn[:], self.zs[:, h, :], ALU.mult, [kr, "zs%d" % h], ["oab%d" % h])

        for g in range(4):
            b = self.ps()
            pst = self.psum[b]
            for tb in range(4):
                self.mm(pst[:, tb * P:(tb + 1) * P], self.sgv[:, tb, g * P:(g + 1) * P], self.wsTb[:, g, :], True, True,
                        ["sgv%d" % tb, "wsTb"], ["ps%d" % b])
            t0 = self.tmp[g % 2]
            k0 = "tmp%d" % (g % 2)
            bias = self.prow[:, 1024 + g * P:1024 + (g + 1) * P]
            for tb in range(4):
                self.tt("dve", t0[:, tb * P:(tb + 1) * P], pst[:, tb * P:(tb + 1) * P], bias, ALU.add, ["ps%d" % b, "prow"], [k0])
            self.tt("pool", self.oab[:, 4 + g, :], t0[:], self.ug[:, g, :], ALU.mult, [k0, "ug%d" % g], ["oab%d" % (4 + g)])

        for j in range(2):
            slot = getw(B_AB + j)
            wt = self.wring[slot]
            for m in range(4):
                c = 4 * j + m
                ba_ = self.ps()
                bb_ = self.ps()
                pa, pb = self.psum[ba_], self.psum[bb_]
                for kc in range(4):
                    self.mm(pa[:], wt[:, kc, m * P:(m + 1) * P], self.oab[:, kc, :], kc == 0, kc == 3,
                            ["wr%d" % slot, "oab%d" % kc], ["ps%d" % ba_])
                for kc in range(4):
                    self.mm(pb[:], wt[:, 4 + kc, m * P:(m + 1) * P], self.oab[:, 4 + kc, :], kc == 0, kc == 3,
                            ["wr%d" % slot, "oab%d" % (4 + kc)], ["ps%d" % bb_])
                t0 = self.tmp[c % 2]
                k0 = "tmp%d" % (c % 2)
                t1 = self.tmp[2 + c % 2]
                k1 = "tmp%d" % (2 + c % 2)
                self.tt("dve", t0[:], pa[:], self.gates[:, c, :], ALU.mult, ["ps%d" % ba_, "gt%d" % c], [k0])
                self.tt("dve", t1[:], pb[:], self.gates[:, 8 + c, :], ALU.mult, ["ps%d" % bb_, "gt%d" % (8 + c)], [k1])
                self.tt("pool", actb[:, c, :], t0[:], t1[:], ALU.add, [k0, k1], ["ab%d" % c])

        self.proj_res_ln(B_OUT, getw, 8, actb, xT, "xT", y, "y", L1G, L1B, actb, "ab")
        if self.pp and s + 1 < self.n_steps:
            self.dma("sp", xT[:], self.xin[:, tok0 + T:tok0 + 2 * T].rearrange("(c p) t -> p c t", p=P), (),
                     ["xT%d" % c for c in range(8)], "d_x")

        for j in range(11):
            slot = getw(B_UP + j)
            wt = self.wring[slot]
            accs = []
            for m in range(4):
                ci = 4 * j + m
                b = self.ps()
                pst = self.psum[b]
                for kc in range(8):
                    self.mm(pst[:], wt[:, kc, m * P:(m + 1) * P], actb[:, kc, :], kc == 0, kc == 7,
                            ["wr%d" % slot, "ab%d" % kc], ["ps%d" % b])
                self.conv_evac(pst, b, ci, 3, self.halof, pcol[:, CF + 3 * ci:CF + 3 * ci + 3], "f")
                accs.append(self.lastacc)
                if m >= 2:
                    an = accs[m - 2]
                    bn = accs[m]
                    sa = self.tmp[m % 2]
                    ks = "tmp%d" % (m % 2)
                    self.act(sa[:], self.acc[an][:], AF.Silu, ["acc%d" % an], [ks])
                    hi = 2 * j + (m - 2)
                    self.tt("pool", self.hid[:, hi, :], sa[:], self.acc[bn][:], ALU.mult, [ks, "acc%d" % bn], ["qkv%d" % (hi // 2)])

        if self.pp:
            self.recv_next = (s + 1) if (2 <= s + 1 < self.n_steps) else None
            self.proj_res_ln(B_DN, getw, 22, self.hid, y, "y", y, "y", L2G, L2B, None, None)
            outbuf, outk_ = y, "y"
        else:
            self.proj_res_ln(B_DN, getw, 22, self.hid, y, "y", xT, "xT", L2G, L2B, None, None)
            outbuf, outk_ = xT, "xT"
        assert nxt[0] == NBLK and not pending
        if l == self.depth - 1:
            self.dma("sp", self.yout[:, tok0:tok0 + T].rearrange("(c p) t -> p c t", p=P), outbuf[:],
                     ["%s%d" % (outk_, c) for c in range(8)], ["yout"], "d_y")
        if self.pp:
            if s < self.n_steps - 2:
                par = s % 2
                self.dma("sp", self.snd[par].ap().rearrange("(c p) t -> p c t", p=P), outbuf[:],
                         ["%s%d" % (outk_, c) for c in range(8)], ["snd%d" % par], "d_snd")
                self.pending_cc = par
            if s < 2:
                kc_ = self.flg[:, 2 + s:3 + s]
                fl2 = lambda t: t[:].rearrange("p a b -> p (a b)")
                hq = ["haloq%d_%d" % (c, l) for c in range(12)]
                hf = ["halof%d_%d" % (c, l) for c in range(44)]
                self.ts("dve", fl2(self.haloq), fl2(self.haloq), kc_, None, ALU.mult, None, hq + ["flg"], hq)
                self.ts("dve", fl2(self.halof), fl2(self.halof), kc_, None, ALU.mult, None, hf + ["flg"], hf)
                self.ts("dve", fl2(self.S), fl2(self.S), kc_, None, ALU.mult, None, ["S", "flg"], ["S"])
                self.ts("dve", fl2(self.Sb), fl2(self.Sb), kc_, None, ALU.mult, None, ["Sb", "flg"], ["Sb"])

    def fill(self, every=2):
        sg = getattr(self, "scan_gen", None)
        if sg is not None:
            try:
                next(sg)
            except StopIteration:
                self.scan_gen = None
        if self.filler is None:
            return
        self.fill_ctr += 1
        if self.fill_ctr % every:
            return
        try:
            next(self.filler)
        except StopIteration:
            self.filler = None

    def gelu(self, src, b, out, wkeys):
        self.act(out, src, AF.Gelu_apprx_tanh, ["ps%d" % b], wkeys)

    def rsqrt(self, out, src, epsi, scale, r, w):
        self.act(out, src, AF.Sqrt, r + ["epsc"], w, bias=self.epsc[:, epsi:epsi + 1], scale=scale)
        self.s.add("dve", lambda e: e.reciprocal(out=out, in_=out), w, w)

    def conv_evac(self, pst, b, ci, K, halo, wcols, tag):
        n = self.rawn % 3
        self.rawn += 1
        raw = self.raw[n]
        kr = "raw%d" % n
        acc = self.acc[n]
        ka = "acc%d" % n
        hk = "halo%s%d_%d" % (tag, ci, self.cur_l)
        H = K - 1
        self.cp("act", raw[:, 4:4 + T], pst[:], ["ps%d" % b], [kr + "m"])
        self.cp("pool", raw[:, 4 - H:4], halo[:, ci, 4 - H:4], [hk], [kr + "h"])
        rk = [kr + "m", kr + "h"]
        self.act(acc[:], pst[:], AF.Copy, ["ps%d" % b, "pcol"], [ka], scale=wcols[:, K - 1:K])
        for jj in range(1, K - 1):
            sh = (K - 1) - jj
            self.stt("dve", acc[:], raw[:, 4 - sh:4 - sh + T], wcols[:, jj:jj + 1], acc[:], ALU.mult, ALU.add,
                     rk + ["pcol", ka], [ka])
        sh = K - 1
        self.ts("pool", raw[:, 4 - sh:4 - sh + T], raw[:, 4 - sh:4 - sh + T], wcols[:, 0:1], None, ALU.mult, None,
                rk + ["pcol"], rk) if False else None
        self.stt("dve", acc[:], raw[:, 4 - sh:4 - sh + T], wcols[:, 0:1], acc[:], ALU.mult, ALU.add,
                 rk + ["pcol", ka], [ka])
        self.cp("pool", halo[:, ci, 4 - H:4], raw[:, 4 + T - H:4 + T], rk, [hk])
        self.lastacc = n

    def proj_res_ln(self, blk0, getw, nkc, src, res, resk, out, outk, gcol, bcol, outb, outbk):
        pcol = self.pcol
        cstb = self.cstb
        onesb = cstb[:, C_ONE, :]
        srck = "ab" if src is self.actb else "hid"
        nparts = (nkc + 7) // 8
        r = out
        for j in range(2):
            banks = [self.ps() for _ in range(4)]
            for part in range(nparts):
                slot = getw(blk0 + j * nparts + part)
                wt = self.wring[slot]
                k0 = part * 8
                k1 = min(nkc, k0 + 8)
                for m in range(4):
                    pst = self.psum[banks[m]]
                    for kc in range(k0, k1):
                        sk = ("ab%d" % kc) if srck == "ab" else ("qkv%d" % (kc // 2))
                        self.mm(pst[:], wt[:, kc - k0, m * P:(m + 1) * P], src[:, kc, :], kc == 0, kc == nkc - 1,
                                ["wr%d" % slot, sk], ["ps%d" % banks[m]])
            for m in range(4):
                c = 4 * j + m
                pst = self.psum[banks[m]]
                self.stt("dve", r[:, c, :], res[:, c, :], ALPHA, pst[:], ALU.mult, ALU.add,
                         ["%s%d" % (resk, c), "ps%d" % banks[m]], ["%s%d" % (outk, c)])
        if getattr(self, "recv_next", None) is not None and src is self.hid:
            sn = self.recv_next
            self.recv_next = None
            g = self.gat[sn % 2]
            self.dma("sp", self.qkv[:, 0:8, :], g.ap()[0:D, :].rearrange("(c p) t -> p c t", p=P), ["gat%d" % (sn % 2)],
                     ["qkv%d" % c for c in range(8)], "d_rcv")
        bm = self.ps()
        bq = self.ps()
        pm, pq = self.psum[bm], self.psum[bq]
        for c in range(8):
            rb = self.lnb[:, c % 2, :]
            rq = self.lnb[:, 2 + c % 2, :]
            kb_, kq_ = "lnb%d" % (c % 2), "lnb%d" % (2 + c % 2)
            self.cp("pool", rb, r[:, c, :], ["%s%d" % (outk, c)], [kb_])
            self.act(rq, r[:, c, :], AF.Square, ["%s%d" % (outk, c)], [kq_])
            self.mm(pm[:], onesb, rb, c == 0, c == 7, ["cstb", kb_], ["ps%d" % bm])
            self.mm(pq[:], onesb, rq, c == 0, c == 7, ["cstb", kq_], ["ps%d" % bq])
        st = self.stat
        self.ts("dve", st[:, 0, :], pm[:], 1.0 / D, None, ALU.mult, None, ["ps%d" % bm], ["osb0"])
        self.tt("dve", st[:, 2, :], st[:, 0, :], st[:, 0, :], ALU.mult, ["osb0"], ["osb2"])
        self.stt("dve", st[:, 1, :], pq[:], 1.0 / D, st[:, 2, :], ALU.mult, ALU.subtract, ["ps%d" % bq, "osb2"], ["osb1"])
        self.rsqrt(st[:, 1, :], st[:, 1, :], 0, 1.0, ["osb1"], ["osb1"])
        self.stt("dve", st[:, 2, :], st[:, 0, :], -1.0, st[:, 1, :], ALU.mult, ALU.mult, ["osb0", "osb1"], ["osb2"])
        for c in range(8):
            t0 = self.tmp[c % 4]
            k0 = "tmp%d" % (c % 4)
            kk = "%s%d" % (outk, c)
            self.tt("dve", t0[:], r[:, c, :], st[:, 1, :], ALU.mult, [kk, "osb1"], [k0])
            self.tt("dve" if c % 2 == 0 else "pool", t0[:], t0[:], st[:, 2, :], ALU.add, [k0, "osb2"], [k0])
            gc_ = pcol[:, gcol + c:gcol + c + 1]
            bc_ = pcol[:, bcol + c:bcol + c + 1]
            self.act(out[:, c, :], t0[:], AF.Identity, [k0, "pcol"], [kk], bias=bc_, scale=gc_)
            if outb is not None:
                self.act(outb[:, c, :], t0[:], AF.Identity, [k0, "pcol"], ["%s%d" % (outbk, c)], bias=bc_, scale=gc_)

    def dn_masks(self, tb):
        cst, cstb = self.cst, self.cstb
        ident = cst[:, C_ID, :]
        identb = cstb[:, C_ID, :]
        triu = cst[:, C_TRIU, :]
        cols = self.cols
        kcols = ["colb", "collb", "colg", "colng", "colG", "colk", "colkd"]
        pr = tb % 2
        dec, deck = self.dec[pr], self.deck[pr]
        fl4 = lambda ap: ap.rearrange("p a b -> p (a b)")

        def bc(kind, h):
            return cols[:, kind, tb, h:h + 1].to_broadcast([P, P])

        bU, bA, bL, bG = self.ps(), self.ps(), self.ps(), self.ps()
        pU, pA, pL, pG = self.psum[bU], self.psum[bA], self.psum[bL], self.psum[bG]
        for h in range(4):
            hs = slice(h * P, (h + 1) * P)
            self.mm(pU[:, hs], bc(2, h), triu, True, False, kcols + ["cst"], ["ps%d" % bU])
            self.mm(pU[:, hs], triu, bc(3, h), False, False, kcols + ["cst"], ["ps%d" % bU])
            self.mm(pU[:, hs], bc(1, h), ident, False, False, kcols + ["cst"], ["ps%d" % bU])
            self.mm(pU[:, hs], identb, cstb[:, C_MUS, :], False, True, ["cstb"], ["ps%d" % bU])
            self.mm(pA[:, hs], bc(2, h), triu, True, False, kcols + ["cst"], ["ps%d" % bA])
            self.mm(pA[:, hs], triu, bc(3, h), False, False, kcols + ["cst"], ["ps%d" % bA])
            self.mm(pA[:, hs], identb, cstb[:, C_MUI, :], False, True, ["cstb"], ["ps%d" % bA])
            self.mm(pL[:, hs], triu, bc(2, h), True, False, kcols + ["cst"], ["ps%d" % bL])
            self.mm(pL[:, hs], bc(3, h), triu, False, False, kcols + ["cst"], ["ps%d" % bL])
            self.mm(pL[:, hs], ident, bc(1, h), False, False, kcols + ["cst"], ["ps%d" % bL])
            self.mm(pL[:, hs], identb, cstb[:, C_NLS, :], False, True, ["cstb"], ["ps%d" % bL])
            self.mm(pG[:, hs], bc(2, h), triu, True, True, kcols + ["cst"], ["ps%d" % bG])

        self.act(fl4(dec["DUs"]), pU[:], AF.Exp, ["ps%d" % bU], deck["DUs"])
        self.act(fl4(dec["DUi"]), pA[:], AF.Exp, ["ps%d" % bA], deck["DUi"])
        self.act(fl4(dec["DLs"]), pL[:], AF.Exp, ["ps%d" % bL], deck["DLs"])
        self.act(fl4(dec["EG"]), pG[:], AF.Exp, ["ps%d" % bG], deck["EG"])


    def deltanet_block(self, tb):
        cst, cstb = self.cst, self.cstb
        ident = cst[:, C_ID, :]
        identb = cstb[:, C_ID, :]
        triu = cst[:, C_TRIU, :]
        cols = self.cols
        dn = self.dn
        qkv = self.qkv
        blk = slice(tb * P, (tb + 1) * P)
        kcols = ["colb", "collb", "colg", "colng", "colG", "colk", "colkd"]
        pr = tb % 2
        d2 = self.dn2[pr]

        def bc(kind, h):
            return cols[:, kind, tb, h:h + 1].to_broadcast([P, P])

        def fl(t):
            return t[:].rearrange("p a b -> p (a b)")

        dec, deck = self.dec[pr], self.deck[pr]
        fl4 = lambda ap: ap.rearrange("p a b -> p (a b)")

        qkb = self.qkb
        self.cp("act", qkb[:], qkv[:, 0:8, blk], ["qkv%d" % i for i in range(8)], ["qkb"])
        bK, bQ = self.ps(), self.ps()
        pK, pQ = self.psum[bK], self.psum[bQ]
        for h in range(4):
            hs = slice(h * P, (h + 1) * P)
            self.mm(pK[:, hs], qkb[:, 4 + h, :], qkb[:, 4 + h, :], True, True, ["qkb"], ["ps%d" % bK])
            self.mm(pQ[:, hs], qkb[:, 4 + h, :], qkb[:, h, :], True, True, ["qkb"], ["ps%d" % bQ])
        self.fill(1)
        self.stt("dve", fl(dn["Y"]), pK[:], -1.0, fl4(dec["DUs"]), ALU.mult, ALU.mult, ["ps%d" % bK] + deck["DUs"], ["Y"])
        self.stt("dve", fl(dn["YT"]), pK[:], -1.0, fl4(dec["DLs"]), ALU.mult, ALU.mult, ["ps%d" % bK] + deck["DLs"], ["YT"])
        self.tt("dve", fl(d2["Aqk"]), pQ[:], fl4(dec["DUi"]), ALU.mult, ["ps%d" % bQ] + deck["DUi"], ["Aqk%d" % pr])
        self.tt("pool", dn["Pm"][:], dn["Y"][:], cstb[:, C_ID:C_ID + 1, :].to_broadcast([P, 4, P]), ALU.add, ["Y", "cstb"], ["Pm"])
        self.tt("pool", d2["qg"][:], qkv[:, 0:4, blk], dec["EG"], ALU.mult, ["qkv0", "qkv1", "qkv2", "qkv3"] + deck["EG"], ["qg%d" % pr])
        self.cp("pool", d2["gl"][:, :, 0:1], dec["EG"][:, :, 63:64], deck["EG"], ["gl%d" % pr])
        self.cp("pool", d2["gl"][:, :, 1:2], dec["EG"][:, :, 127:128], deck["EG"], ["gl%d" % pr])
        if tb + 1 < 4:
            self.dn_masks(tb + 1)

        bT, bV = self.ps(), self.ps()
        pT, pV = self.psum[bT], self.psum[bV]
        for h in range(4):
            hs = slice(h * P, (h + 1) * P)
            self.tr(pT[:, hs], qkv[:, 4 + h, blk], ident, ["qkv%d" % (4 + h), "cst"], ["ps%d" % bT])
            self.tr(pV[:, hs], qkv[:, 8 + h, blk], ident, ["qkv%d" % (8 + h), "cst"], ["ps%d" % bV])
        self.fill(1)

        def cb(kind):
            return cols[:, kind, tb, :].rearrange("p (h o) -> p h o", o=1).to_broadcast([P, 4, P])

        p3 = lambda ap: ap.rearrange("p (a b) -> p a b", a=4)
        self.tt("dve", dn["Kbg"][:], p3(pT[:]), cb(6), ALU.mult, ["ps%d" % bT] + kcols, ["Kbg"])
        self.tt("dve", d2["kd"][:], p3(pT[:]), cb(7), ALU.mult, ["ps%d" % bT] + kcols, ["kd%d" % pr])
        self.tt("dve", dn["Vb"][:], p3(pV[:]), cb(0), ALU.mult, ["ps%d" % bV] + kcols, ["Vb"])

        Z, ZT, Z2, ZT2 = "Y", "YT", "Z", "ZT"
        for lev in range(1, 6):
            last = lev == 5
            bz, bzt = self.ps(), self.ps()
            pz, pzt = self.psum[bz], self.psum[bzt]
            for h in range(4):
                hs = slice(h * P, (h + 1) * P)
                if not last:
                    self.mm(pz[:, hs], dn[ZT][:, h, :], dn[Z][:, h, :], True, True, [Z, ZT], ["ps%d" % bz])
                self.mm(pzt[:, hs], dn[Z][:, h, :], dn[ZT][:, h, :], True, True, [Z, ZT], ["ps%d" % bzt])
            self.fill(2)
            if not last:
                self.cp("act", fl(dn[Z2]), pz[:], ["ps%d" % bz], [Z2])
            self.cp("act", fl(dn[ZT2]), pzt[:], ["ps%d" % bzt], [ZT2])
            bp = self.ps()
            pp = self.psum[bp]
            for h in range(4):
                hs = slice(h * P, (h + 1) * P)
                self.mm(pp[:, hs], dn[ZT2][:, h, :], dn["Pm"][:, h, :], True, True, [ZT2, "Pm"], ["ps%d" % bp])
            self.fill(2)
            self.tt("dve", fl(dn["Pm"]), fl(dn["Pm"]), pp[:], ALU.add, ["Pm", "ps%d" % bp], ["Pm"])
            if Z == "Y":
                Z, ZT, Z2, ZT2 = "Z", "ZT", "Z2", "ZT2"
            else:
                Z, ZT, Z2, ZT2 = Z2, ZT2, Z, ZT

        bW, bUu = self.ps(), self.ps()
        pW, pUu = self.psum[bW], self.psum[bUu]
        for h in range(4):
            hs = slice(h * P, (h + 1) * P)
            self.mm(pW[:, hs], dn["Kbg"][:, h, :], dn["Pm"][:, h, :], True, True, ["Kbg", "Pm"], ["ps%d" % bW])
            self.mm(pUu[:, hs], dn["Pm"][:, h, :], dn["Vb"][:, h, :], True, True, ["Pm", "Vb"], ["ps%d" % bUu])
        self.fill(1)
        self.cp("act", fl(d2["WT"]), pW[:], ["ps%d" % bW], ["WT%d" % pr])
        self.cp("act", fl(d2["U"]), pUu[:], ["ps%d" % bUu], ["U%d" % pr])

    def dn_scan(self, tb):
        pr = tb % 2
        d2 = self.dn2[pr]
        blk = slice(tb * P, (tb + 1) * P)
        fl = lambda t: t[:].rearrange("p a b -> p (a b)")
        p3 = lambda ap: ap.rearrange("p (a b) -> p a b", a=4)
        bO, bws, bs = 5, 6, 7
        pO, pws, pS = self.psum[bO], self.psum[bws], self.psum[bs]
        S = self.S
        Sb = self.Sb
        for cc in range(2):
            rows = slice(cc * 64, (cc + 1) * 64)
            for h in range(4):
                hs = slice(h * P, (h + 1) * P)
                self.mm(pws[:, hs], d2["WT"][:, h, :], Sb[:, h, :], True, True, ["WT%d" % pr, "Sb"], ["ps%d" % bws])
            yield
            self.tt("dve", d2["un"][rows, :, :], d2["U"][rows, :, :], p3(pws[rows, :]), ALU.subtract,
                    ["U%d" % pr, "ps%d" % bws], ["un%d" % pr])
            yield
            for h in range(4):
                oc = slice(h * P + cc * 64, h * P + (cc + 1) * 64)
                self.mm(pO[:, oc], Sb[:, h, :], d2["qg"][:, h, rows], True, False, ["Sb", "qg%d" % pr], ["ps%d" % bO])
                self.mm(pO[:, oc], d2["un"][rows, h, :], d2["Aqk"][rows, h, rows], False, True,
                        ["un%d" % pr, "Aqk%d" % pr], ["ps%d" % bO])
                hs = slice(h * P, (h + 1) * P)
                self.mm(pS[:, hs], d2["kd"][rows, h, :], d2["un"][rows, h, :], True, True,
                        ["kd%d" % pr, "un%d" % pr], ["ps%d" % bs])
            glb = d2["gl"][:, :, cc:cc + 1].to_broadcast([P, 4, P])
            self.tt("pool", S[:], S[:], glb, ALU.mult, ["S", "gl%d" % pr], ["S"])
            yield
            self.tt("dve", fl(S), fl(S), pS[:], ALU.add, ["S", "ps%d" % bs], ["S"])
            self.cp("act", fl(Sb), fl(S), ["S"], ["Sb"])
            yield
        self.cp("act", self.osb[:, :, blk], p3(pO[:]), ["ps%d" % bO], ["osb0", "osb1", "osb2", "osb3"])


def _consts():
    idx = np.arange(P)
    same = (idx[:, None] // 64) == (idx[None, :] // 64)
    c = np.zeros((P, 8, P), np.float32)
    c[:, C_ID, :] = np.eye(P, dtype=np.float32)
    c[:, C_TRIU, :] = ((idx[:, None] <= idx[None, :]) & same)
    c[:, C_BLK, :] = same
    c[:, C_MUS, :] = np.where((idx[None, :] > idx[:, None]) & same, 0.0, -BIG)
    c[:, C_MUI, :] = np.where((idx[None, :] >= idx[:, None]) & same, 0.0, -BIG)
    c[:, C_NLS, :] = np.where((idx[None, :] < idx[:, None]) & same, 0.0, -BIG)
    c[:, C_SPU, :] = (idx[:, None] <= idx[None, :])
    c[:, C_ONE, :] = 1.0
    return np.ascontiguousarray(c.reshape(P, 8 * P))


def _blockify(w):
    K = w.shape[0]
    out = np.zeros((P, 8, 512), np.float32)
    nk = K // P
    out[:, :nk, :] = w.reshape(nk, P, 512).transpose(1, 0, 2)
    return out


def prep_layer(inp, l):
    w_in = np.asarray(inp["w_in"][l])
    qkvz = w_in[:, 0:2048]
    ba = w_in[:, 2048:2056]
    u = w_in[:, 2056:2568]
    sv = w_in[:, 2568:3080]
    gates = w_in[:, 3080:5128]
    fm = np.concatenate([qkvz, u, gates], axis=1)
    blocks = [_blockify(fm[:, j * 512:(j + 1) * 512]) for j in range(9)]
    blocks.append(_blockify(sv))
    wa = np.asarray(inp["w_branch_a"][l])
    wb = np.asarray(inp["w_branch_b"][l])
    for j in range(2):
        blocks.append(_blockify(np.concatenate([wa[:, j * 512:(j + 1) * 512], wb[:, j * 512:(j + 1) * 512]], axis=0)))
    wo = np.asarray(inp["w_out"][l])
    for j in range(2):
        blocks.append(_blockify(wo[:, j * 512:(j + 1) * 512]))
    wu = np.asarray(inp["w_up"][l])
    for j in range(11):
        blocks.append(_blockify(np.concatenate([wu[:, j * 256:(j + 1) * 256], wu[:, FFN + j * 256:FFN + (j + 1) * 256]], axis=1)))
    wd = np.asarray(inp["w_down"][l])
    for j in range(2):
        for part in range(3):
            k0 = part * 1024
            k1 = min(FFN, k0 + 1024)
            blocks.append(_blockify(wd[k0:k1, j * 512:(j + 1) * 512]))
    wblk = np.stack(blocks).reshape(NBLK, P, 4096)
    wba = np.ascontiguousarray(ba.reshape(8, P, 8).transpose(1, 0, 2).reshape(P, 64))
    pcol = np.zeros((P, NPC), np.float32)
    cq = np.asarray(inp["conv_qkv"][l])
    pcol[:, CQ:CQ + 48] = cq.reshape(4, 12, P).transpose(2, 1, 0).reshape(P, 48)
    cf = np.asarray(inp["conv_ffn"][l])
    cfa = cf[:, :FFN].reshape(3, 22, P)
    cfb = cf[:, FFN:].reshape(3, 22, P)
    cfo = np.zeros((P, 44, 3), np.float32)
    for j in range(11):
        for m in range(2):
            cfo[:, 4 * j + m, :] = cfa[:, 2 * j + m, :].T
            cfo[:, 4 * j + 2 + m, :] = cfb[:, 2 * j + m, :].T
    pcol[:, CF:CF + 132] = cfo.reshape(P, 132)
    pcol[:, NW] = np.asarray(inp["dn_norm_w"][l])
    pcol[:, L1G:L1G + 8] = np.asarray(inp["ln1_g"][l]).reshape(8, P).T
    pcol[:, L1B:L1B + 8] = np.asarray(inp["ln1_b"][l]).reshape(8, P).T
    pcol[:, L2G:L2G + 8] = np.asarray(inp["ln2_g"][l]).reshape(8, P).T
    pcol[:, L2B:L2B + 8] = np.asarray(inp["ln2_b"][l]).reshape(8, P).T
    pcol[:, ALOG:ALOG + 4] = np.asarray(inp["a_log"][l])[None, :]
    pcol[:, DTB:DTB + 4] = np.asarray(inp["dt_bias"][l])[None, :]
    prow = np.zeros((P, 1536), np.float32)
    prow[:, 0:512] = np.asarray(inp["sg_ln_g"][l])[None, :]
    prow[:, 512:1024] = np.asarray(inp["sg_ln_b"][l])[None, :]
    prow[:, 1024:1536] = np.asarray(inp["b_spatial"][l]).reshape(1, 512)
    ws = np.asarray(inp["w_spatial"][l])
    wsT = np.ascontiguousarray(ws.transpose(2, 0, 1).reshape(P, 512))
    return {"wblk": np.ascontiguousarray(wblk), "wba": wba, "pcol": pcol, "prow": prow, "wsT": wsT, "cst": _consts()}


_NC_CACHE = {}


def get_nc(n_steps, depth, pp=False):
    if (n_steps, depth, pp) not in _NC_CACHE:
        _NC_CACHE[(n_steps, depth, pp)] = Builder(n_steps, depth, pp).build()
    return _NC_CACHE[(n_steps, depth, pp)]


def run_pp(xT_list, lays, n_tiles):
    n_steps = n_tiles + 2
    nc = get_nc(n_steps, 1, True)
    in_maps = []
    for b, xT in enumerate(xT_list):
        for role in range(2):
            lay = lays[role]
            m = {"cst": lay["cst"]}
            for k in ("wblk", "wba", "pcol", "prow", "wsT"):
                m["%s0" % k] = lay[k]
            xin = np.zeros((D, n_steps * T), np.float32)
            flag = np.ones((P, 2 + n_steps), np.float32)
            if role == 0:
                xin[:, :n_tiles * T] = xT
                flag[:, 0] = 0.0
            else:
                flag[:, 2:4] = 0.0
            m["xin"] = xin
            m["flag"] = flag
            in_maps.append(m)
    res = run_bass_kernel_spmd(nc, in_maps, core_ids=list(range(len(in_maps))))
    return [res.results[2 * b + 1]["yout"][:, 2 * T:] for b in range(len(xT_list))]


def run_layers(xT_list, lays, n_steps):
    nc = get_nc(n_steps, len(lays))
    base = {"cst": lays[0]["cst"]}
    for l, lay in enumerate(lays):
        for k in ("wblk", "wba", "pcol", "prow", "wsT"):
            base["%s%d" % (k, l)] = lay[k]
    in_maps = []
    for xT in xT_list:
        m = dict(base)
        m["xin"] = np.ascontiguousarray(xT, dtype=np.float32)
        in_maps.append(m)
    res = run_bass_kernel_spmd(nc, in_maps, core_ids=list(range(len(xT_list))))
    return [r["yout"] for r in res.results]


def kernel(**inputs):
    x = np.asarray(inputs["x"], dtype=np.float32)
    n_steps = SEQ // T
    cur = [np.ascontiguousarray(x[b].T) for b in range(BATCH)]
    lays = [prep_layer(inputs, l) for l in range(DEPTH)]
    cur = run_pp(cur, lays, n_steps)
    out = np.stack([c.T for c in cur]).astype(np.float32)
    return out
```

```python
import numpy as np
from contextlib import ExitStack
import concourse.bass as bass
import concourse.mybir as mybir
from concourse.bass_utils import run_bass_kernel_spmd

F32 = mybir.dt.float32
BF16 = mybir.dt.bfloat16
AF = mybir.ActivationFunctionType
ALU = mybir.AluOpType

P = 128
T = 512
D = 1024
KC = 8
DEPTH = 2
SEQ = 8192
BATCH = 4
FFN = 2816
NBLK = 31
B_IN, B_AB, B_OUT, B_UP, B_DN = 0, 10, 12, 14, 25
ALPHA = (2 * DEPTH) ** 0.25
LN_EPS = 1e-5
RMS_EPS = 1e-6
L2_EPS = 1e-6
BIG = 32768.0
CQ, CF, NW, L1G, L1B, L2G, L2B, ALOG, DTB, NPC = 0, 48, 180, 181, 189, 197, 205, 213, 217, 221
C_ID, C_TRIU, C_BLK, C_MUS, C_MUI, C_NLS, C_SPU, C_ONE = range(8)
GELU_C = 1.5957691216057308


class _Op:
    __slots__ = ("eng", "fn", "deps", "signal", "tick", "semkey", "isdma", "raw", "inc")

    def __init__(self, eng, fn, isdma, semkey, inc=None):
        self.inc = inc if inc is not None else (16 if isdma else 1)
        self.eng = eng
        self.fn = fn
        self.deps = []
        self.signal = False
        self.tick = 0
        self.semkey = semkey
        self.isdma = isdma


class Sched:
    ENGS = ("pe", "act", "dve", "pool", "sp")

    def __init__(self):
        self.ops = {e: [] for e in self.ENGS}
        self.last_w = {}
        self.readers = {}
        self.all = []
        self.alias = {}

    def add(self, eng, fn, reads=(), writes=(), dma=False, semkey=None, inc=None):
        op = _Op(eng, fn, dma, semkey if dma else eng, inc)
        reads = [self.alias.get(k, k) for k in reads]
        writes = [self.alias.get(k, k) for k in writes]
        deps = {}
        for k in reads:
            w = self.last_w.get(k)
            if w is not None:
                deps[id(w)] = (w, True)
        for k in writes:
            w = self.last_w.get(k)
            if w is not None and id(w) not in deps:
                deps[id(w)] = (w, False)
            for r in self.readers.get(k, ()):
                if id(r) not in deps:
                    deps[id(r)] = (r, False)
        for d, israw in deps.values():
            if d.isdma:
                op.deps.append(d)
            elif d.eng == eng and not dma:
                if eng != "pe" and israw:
                    op.deps.append(d)
            else:
                op.deps.append(d)
        for d in op.deps:
            d.signal = True
        for k in reads:
            self.readers.setdefault(k, []).append(op)
        for k in writes:
            self.last_w[k] = op
            self.readers[k] = []
        self.ops[eng].append(op)
        self.all.append(op)
        return op

    def emit(self, nc, es):
        counts = {}
        for op in self.all:
            if op.signal:
                counts[op.semkey] = counts.get(op.semkey, 0) + op.inc
                op.tick = counts[op.semkey]
        sems = {}
        for k in counts:
            sems[k] = es.enter_context(nc.semaphore("s_" + str(k)))
        block = es.enter_context(nc.Block())

        def run(engname):
            def body(e):
                waited = {}
                for op in self.ops[engname]:
                    need = {}
                    for d in op.deps:
                        if need.get(d.semkey, 0) < d.tick:
                            need[d.semkey] = d.tick
                    for k, v in need.items():
                        if waited.get(k, 0) < v:
                            e.wait_ge(sems[k], v)
                            waited[k] = v
                    ins = op.fn(e)
                    if op.signal:
                        if op.isdma and op.inc == 1:
                            ins.then_inc(sems[op.semkey])
                        else:
                            ins.then_inc(sems[op.semkey], op.inc)
            return body

        block.tensor(run("pe"))
        block.scalar(run("act"))
        block.vector(run("dve"))
        block.gpsimd(run("pool"))
        block.sync(run("sp"))


class Builder:
    def __init__(self, n_steps, depth=1, pp=False):
        self.n_steps = n_steps
        self.depth = depth
        self.pp = pp
        self.pending_cc = None
        self.nc = bass.Bass("TRN2", target_bir_lowering=False)
        self.s = Sched()
        self.psn = 0
        self.wn = 0
        self.rawn = 0

    def mm(self, out, lhsT, rhs, start, stop, r, w):
        self.s.add("pe", lambda e: e.matmul(out, lhsT=lhsT, rhs=rhs, start=start, stop=stop), r, w)

    def tr(self, out, in_, ident, r, w):
        self.s.add("pe", lambda e: e.transpose(out, in_, ident), r, w)

    def act(self, out, in_, func, r, w, bias=None, scale=None):
        kw = {}
        if bias is not None:
            kw["bias"] = bias
        if scale is not None:
            kw["scale"] = scale
        self.s.add("act", lambda e: e.activation(out=out, in_=in_, func=func, **kw), r, w)

    def tt(self, eng, out, in0, in1, alu, r, w):
        self.s.add(eng, lambda e: e.tensor_tensor(out=out, in0=in0, in1=in1, op=alu), r, w)

    def ts(self, eng, out, in0, s1, s2, op0, op1, r, w):
        if op1 is None:
            self.s.add(eng, lambda e: e.tensor_scalar(out=out, in0=in0, scalar1=s1, scalar2=None, op0=op0), r, w)
        else:
            self.s.add(eng, lambda e: e.tensor_scalar(out=out, in0=in0, scalar1=s1, scalar2=s2, op0=op0, op1=op1), r, w)

    def stt(self, eng, out, in0, sc, in1, op0, op1, r, w):
        eng = "dve"
        self.s.add(eng, lambda e: e.scalar_tensor_tensor(out=out, in0=in0, scalar=sc, in1=in1, op0=op0, op1=op1), r, w)

    def cp(self, eng, out, in_, r, w):
        if eng == "act":
            self.s.add("act", lambda e: e.activation(out=out, in_=in_, func=AF.Copy), r, w)
        else:
            self.s.add(eng, lambda e: e.tensor_copy(out=out, in_=in_), r, w)

    def memset(self, eng, ap, val, w):
        self.s.add(eng, lambda e: e.memset(ap, val), (), w)

    def dma(self, eng, out, in_, r, w, semkey):
        self.s.add(eng, lambda e: e.dma_start(out=out, in_=in_), r, w, dma=True, semkey=semkey)

    def ps(self):
        n = getattr(self, "ps_n", 8)
        b = self.psn % n
        self.psn += 1
        return b

    def build(self):
        nc = self.nc
        n_steps = self.n_steps
        ntok = n_steps * T
        dt = nc.dram_tensor
        self.xin = dt("xin", [D, ntok], F32, kind="ExternalInput").ap()
        self.wblk_l = [dt("wblk%d" % l, [NBLK, P, 4096], F32, kind="ExternalInput").ap() for l in range(self.depth)]
        self.wba_l = [dt("wba%d" % l, [P, 64], F32, kind="ExternalInput").ap() for l in range(self.depth)]
        self.pcol_dl = [dt("pcol%d" % l, [P, NPC], F32, kind="ExternalInput").ap() for l in range(self.depth)]
        self.prow_dl = [dt("prow%d" % l, [P, 1536], F32, kind="ExternalInput").ap() for l in range(self.depth)]
        self.wsT_dl = [dt("wsT%d" % l, [P, 512], F32, kind="ExternalInput").ap() for l in range(self.depth)]
        self.cst_d = dt("cst", [P, 1024], F32, kind="ExternalInput").ap()
        self.yout = dt("yout", [D, ntok], F32, kind="ExternalOutput").ap()
        if self.pp:
            self.flag_d = dt("flag", [P, 2 + n_steps], F32, kind="ExternalInput").ap()
            self.snd = [nc.dram_tensor("snd%d" % i, [D, T], F32) for i in range(2)]
            self.gat = [nc.dram_tensor("gat%d" % i, [2 * D, T], F32) for i in range(2)]
        with ExitStack() as es:
            self.es = es
            self.alloc()
            self.prologue()
            for s in range(n_steps):
                for l in range(self.depth):
                    self.step(s, l)
            self.s.add("sp", lambda e: e.nop(), ["yout"], ())
            self.s.emit(nc, es)
        return nc

    def sb(self, name, shape, dtype):
        return self.es.enter_context(self.nc.sbuf_tensor(name, shape, dtype))

    def alloc(self):
        sb = self.sb
        self.xT = sb("xT", [P, 8, T], F32)
        self.y = sb("y", [P, 8, T], F32)
        self.actb = sb("actb", [P, 8, T], BF16)
        self.NW = 3
        self.wring = [sb("wr%d" % i, [P, 8, 512], BF16) for i in range(self.NW)]
        self.raw = [sb("raw%d" % i, [P, 516], F32) for i in range(3)]
        self.acc = [sb("acc%d" % i, [P, T], F32) for i in range(3)]
        self.qkv = sb("qkv", [P, 12, T], F32)
        self.zs = sb("zs", [P, 4, T], BF16)
        self.ug = sb("ug", [P, 4, T], BF16)
        self.gates = sb("gates", [P, 16, T], BF16)
        self.sgv = sb("sgv", [P, 4, T], BF16)
        self.osb = sb("osb", [P, 4, T], F32)
        self.hid = self.qkv[:].rearrange("p a b -> p (a b)").bitcast(BF16).rearrange("p (a b) -> p a b", b=T)
        self.oab = sb("oab", [P, 8, T], BF16)
        self.tmp = [sb("tmp%d" % i, [P, T], F32) for i in range(4)]
        self.lnb = sb("lnb", [P, 4, T], BF16)
        self.tmpb = [self.lnb[:, i, :] for i in range(2)]
        self.s.alias["tmpb0"] = "lnb0"
        self.s.alias["tmpb1"] = "lnb1"
        self.stat = self.osb
        self.haloq_l = [sb("haloq%d" % l, [P, 12, 4], F32) for l in range(self.depth)]
        self.halof_l = [sb("halof%d" % l, [P, 44, 4], F32) for l in range(self.depth)]
        self.S_l = [sb("S%d" % l, [P, 4, P], F32) for l in range(self.depth)]
        self.ba = sb("ba", [P, 4, 8], F32)
        self.cols = sb("cols", [P, 12, 4, 4], F32)
        self.dn = {n: sb("dn_" + n, [P, 4, P], F32) for n in ["DUs", "DUi", "DLs", "EG"]}
        for n in ["Y", "YT", "Pm", "Z", "ZT", "Z2", "ZT2", "Kbg", "Vb"]:
            self.dn[n] = sb("dn_" + n, [P, 4, P], BF16)
        self.qkb = sb("qkb", [P, 8, P], BF16)
        self.dn2 = []
        for i in range(2):
            d = {n: sb("dn%d_%s" % (i, n), [P, 4, P], BF16) for n in ["WT", "qg", "Aqk", "kd", "un"]}
            d["U"] = sb("dn%d_U" % i, [P, 4, P], F32)
            d["gl"] = sb("dn%d_gl" % i, [P, 4, 2], F32)
            self.dn2.append(d)
        self.Sb_l = [sb("Sb%d" % l, [P, 4, P], BF16) for l in range(self.depth)]
        self.small = sb("small", [P, 16, 8], F32)
        self.epsc = sb("epsc", [P, 4], F32)
        if self.pp:
            self.flg = sb("flg", [P, 2 + self.n_steps], F32)
        self.pcol_l = [sb("pcolsb%d" % l, [P, NPC], F32) for l in range(self.depth)]
        self.prow = sb("prowsb", [P, 1536], F32)
        self.cst = sb("cstsb", [P, 3, P], F32)
        self.cstb = sb("cstb", [P, 8, P], BF16)
        self.wsTb_l = [sb("wsTb%d" % l, [P, 4, P], BF16) for l in range(self.depth)]
        self.wbaf = sb("wbaf", [P, 8, 8], F32)
        self.wbab_l = [sb("wbab%d" % l, [P, 8, 8], BF16) for l in range(self.depth)]
        self.ealog_l = [sb("ealog%d" % l, [P, 4], F32) for l in range(self.depth)]
        self.psum = [self.es.enter_context(self.nc.psum_tensor("psb%d" % i, [P, 512], F32)) for i in range(8)]

    def set_layer(self, l):
        self.cur_l = l
        self.pcol, self.wsTb, self.wbab = self.pcol_l[l], self.wsTb_l[l], self.wbab_l[l]
        self.S, self.haloq, self.halof, self.ealog = self.S_l[l], self.haloq_l[l], self.halof_l[l], self.ealog_l[l]
        self.Sb = self.Sb_l[l]
        self.wblk, self.wba, self.pcol_d, self.prow_d, self.wsT_d = (self.wblk_l[l], self.wba_l[l], self.pcol_dl[l],
                                                                       self.prow_dl[l], self.wsT_dl[l])
        for k in ("pcol", "wsTb", "wbab", "S", "Sb", "ealog"):
            self.s.alias[k] = "%s@%d" % (k, l)

    def prologue(self):
        stg = self.y[:, 0:2, :].rearrange("p a b -> p (a b)")
        self.dma("sp", stg, self.cst_d[:, :], (), ["y0", "y1"], "d_cst")
        stg3 = stg.rearrange("p (a b) -> p a b", b=P)
        self.cp("dve", self.cstb[:], stg3, ["y0", "y1"], ["cstb"])
        self.cp("dve", self.cst[:], stg3[:, 0:3, :], ["y0", "y1"], ["cst"])
        self.spu = stg3[:, C_SPU:C_SPU + 1, :]
        self.memset("pool", self.epsc[:, 0:1], LN_EPS, ["epsc"])
        self.memset("pool", self.epsc[:, 1:2], RMS_EPS, ["epsc"])
        self.memset("pool", self.epsc[:, 2:3], L2_EPS, ["epsc"])
        if self.pp:
            self.dma("sp", self.flg[:], self.flag_d[:, :], (), ["flg"], "d_flg")
        for l in range(self.depth):
            self.set_layer(l)
            self.prologue_layer()

    def prologue_layer(self):
        self.dma("sp", self.pcol[:], self.pcol_d[:, :], (), ["pcol"], "d_pcol%d" % self.cur_l)
        wst = self.y[:, 2, :]
        self.dma("sp", wst, self.wsT_d[:, :], (), ["y2"], "d_wsT")
        self.dma("sp", self.wbaf[:].rearrange("p a b -> p (a b)"), self.wba[:, :], (), ["wbaf"], "d_wba")
        self.cp("dve", self.wbab[:], self.wbaf[:], ["wbaf"], ["wbab"])
        self.tt("dve", self.wsTb[:], wst.rearrange("p (a b) -> p a b", b=P), self.spu.to_broadcast([P, 4, P]),
                ALU.mult, ["y2", "y0", "y1"], ["wsTb"])
        self.memset("pool", self.haloq[:], 0.0, ["haloq%d_%d" % (c, self.cur_l) for c in range(12)])
        self.memset("pool", self.halof[:], 0.0, ["halof%d_%d" % (c, self.cur_l) for c in range(44)])
        self.memset("pool", self.S[:], 0.0, ["S"])
        self.memset("pool", self.Sb[:], 0.0, ["Sb"])
        self.act(self.ealog[:], self.pcol[:, ALOG:ALOG + 4], AF.Exp, ["pcol"], ["ealog"])

    def wload(self, blk):
        slot = self.wn % self.NW
        self.wn += 1
        t = self.wring[slot]
        self.dma("pool", t[:].rearrange("p a b -> p (a b)"), self.wblk[blk, :, :], (), ["wr%d" % slot], "d_w%d" % slot)
        return slot

    def step(self, s, l):
        self.set_layer(l)
        tok0 = s * T
        xT, y, actb = self.xT, self.y, self.actb
        pcol = self.pcol
        cst, cstb = self.cst, self.cstb
        ident = cst[:, C_ID, :]
        identb = cstb[:, C_ID, :]
        onesb = cstb[:, C_ONE, :]

        pending = []
        order = [0, 1, 2, 3, 4, 9, 5, 6, 7, 8] + list(range(10, NBLK))
        nxt = [0]

        def prefetch():
            while nxt[0] < NBLK and len(pending) < self.NW:
                pending.append((order[nxt[0]], self.wload(order[nxt[0]])))
                nxt[0] += 1

        def getw(blk):
            prefetch()
            b, slot = pending.pop(0)
            assert b == blk
            return slot

        prefetch()
        self.dma("sp", self.prow[:], self.prow_d[:, :], (), ["prow"], "d_prow")
        if l == 0 and (s == 0 or not self.pp):
            self.dma("sp", xT[:], self.xin[:, tok0:tok0 + T].rearrange("(c p) t -> p c t", p=P), (),
                     ["xT%d" % c for c in range(8)], "d_x")
        if self.pp and s >= 2:
            stg = self.qkv
            for c in range(8):
                self.stt("dve", xT[:, c, :], stg[:, c, :], self.flg[:, 0:1], xT[:, c, :], ALU.mult, ALU.add,
                         ["qkv%d" % c, "xT%d" % c, "flg"], ["xT%d" % c])
        for c in range(8):
            self.cp("act" if c % 2 == 0 else "pool", actb[:, c, :], xT[:, c, :], ["xT%d" % c], ["ab%d" % c])
        abk = ["ab%d" % c for c in range(8)]

        def fm_chunk(wt, slot, c, m):
            b = self.ps()
            pst = self.psum[b]
            for kc in range(8):
                self.mm(pst[:], wt[:, kc, m * P:(m + 1) * P], actb[:, kc, :], kc == 0, kc == 7,
                        ["wr%d" % slot, "ab%d" % kc], ["ps%d" % b])
            if c < 12:
                self.conv_evac(pst, b, c, 4, self.haloq, pcol[:, CQ + 4 * c:CQ + 4 * c + 4], "q")
                accn = self.lastacc
                self.act(self.qkv[:, c, :], self.acc[accn][:], AF.Silu, ["acc%d" % accn], ["qkv%d" % c])
            elif c < 16:
                self.act(self.zs[:, c - 12, :], pst[:], AF.Silu, ["ps%d" % b], ["zs%d" % (c - 12)])
            elif c < 20:
                self.gelu(pst[:], b, self.ug[:, c - 16, :], ["ug%d" % (c - 16)])
            else:
                self.act(self.gates[:, c - 20, :], pst[:], AF.Sigmoid, ["ps%d" % b], ["gt%d" % (c - 20)])

        for j in range(4):
            slot = getw(B_IN + j)
            wt = self.wring[slot]
            for m in range(4):
                fm_chunk(wt, slot, 4 * j + m, m)

        def sgv_unit(wt, slot, tb):
            b = self.ps()
            pst = self.psum[b]
            for kc in range(8):
                self.mm(pst[:], actb[:, kc, tb * P:(tb + 1) * P], wt[:, kc, :], kc == 0, kc == 7,
                        ["wr%d" % slot, "ab%d" % kc], ["ps%d" % b])
            t0 = self.tmp[tb % 2]
            k0 = "tmp%d" % (tb % 2)
            self.gelu(pst[:], b, t0[:], [k0])
            st = self.small[:, 1 + tb, 0:6]
            self.s.add("dve", lambda e, st=st, t0=t0: e.bn_stats(out=st, in_=t0[:]), [k0], ["bnst%d" % tb])
            mv = self.small[:, 5 + tb, 0:2]
            self.s.add("dve", lambda e, st=st, mv=mv: e.bn_aggr(out=mv, in_=st), ["bnst%d" % tb], ["bnmv%d" % tb])
            rs = self.small[:, 5 + tb, 2:3]
            self.rsqrt(rs, self.small[:, 5 + tb, 1:2], 0, 1.0, ["bnmv%d" % tb], ["bnrs%d" % tb])
            self.ts("dve", t0[:], t0[:], self.small[:, 5 + tb, 0:1], rs, ALU.subtract, ALU.mult,
                    [k0, "bnmv%d" % tb, "bnrs%d" % tb], [k0])
            self.tt("pool", t0[:], t0[:], self.prow[:, 0:512], ALU.mult, [k0, "prow"], [k0])
            self.tt("pool", self.sgv[:, tb, :], t0[:], self.prow[:, 512:1024], ALU.add, [k0, "prow"], ["sgv%d" % tb])

        def filler_gen():
            slot = getw(B_IN + 4)
            for m in range(4):
                fm_chunk(self.wring[slot], slot, 16 + m, m)
                yield
            slot = getw(B_IN + 9)
            for tb_ in range(4):
                sgv_unit(self.wring[slot], slot, tb_)
                yield
            for j in range(5, 9):
                slot = getw(B_IN + j)
                for m in range(4):
                    fm_chunk(self.wring[slot], slot, 4 * j + m, m)
                    yield

        self.filler = filler_gen()
        self.fill_ctr = 0
        b = self.ps()
        pst = self.psum[b]
        for tb in range(4):
            for kc in range(8):
                self.mm(pst[:, tb * 8:tb * 8 + 8], actb[:, kc, tb * P:(tb + 1) * P], self.wbab[:, kc, :], kc == 0, kc == 7,
                        ["wbab", "ab%d" % kc], ["ps%d" % b])
        ba = self.ba
        self.cp("act", ba[:].rearrange("p a b -> p (a b)"), pst[:, 0:32], ["ps%d" % b], ["ba"])
        cols = self.cols
        sc4 = self.small[:, 10:12, :].rearrange("p a (b c) -> p (a b) c", c=4)
        sd4 = self.small[:, 12:14, :].rearrange("p a (b c) -> p (a b) c", c=4)
        self.act(sc4, ba[:, :, 0:4], AF.Softplus, ["ba"], ["sc"], scale=-1.0)
        self.tt("dve", sd4, ba[:, :, 4:8], pcol[:, DTB:DTB + 4].rearrange("p (o c) -> p o c", o=1).to_broadcast([P, 4, 4]),
                ALU.add, ["ba", "pcol"], ["sd"])
        self.act(sd4, sd4, AF.Softplus, ["sd"], ["sd"])
        self.act(cols[:, 0, :, :], sc4, AF.Exp, ["sc"], ["colb"], scale=-1.0)
        self.ts("dve", cols[:, 1, :, :], sc4, -1.0, None, ALU.mult, None, ["sc"], ["collb"])
        self.tt("dve", cols[:, 3, :, :], sd4, self.ealog[:].rearrange("p (o c) -> p o c", o=1).to_broadcast([P, 4, 4]),
                ALU.mult, ["sd", "ealog"], ["colng"])
        self.ts("dve", cols[:, 2, :, :], cols[:, 3, :, :], -1.0, None, ALU.mult, None, ["colng"], ["colg"])
        b2 = self.ps()
        p2 = self.psum[b2]
        for tb in range(4):
            self.mm(p2[:, tb * 8:tb * 8 + 4], cst[:, C_TRIU, :], cols[:, 2, tb, :], True, True, ["cst", "colg"], ["ps%d" % b2])
            self.mm(p2[:, tb * 8 + 4:tb * 8 + 8], cst[:, C_BLK, :], cols[:, 2, tb, :], True, True, ["cst", "colg"], ["ps%d" % b2])
        p28 = p2[:, 0:32].rearrange("p (a b) -> p a b", b=8)
        self.cp("dve", cols[:, 4, :, :], p28[:, :, 0:4], ["ps%d" % b2], ["colG"])
        self.tt("dve", sc4, p28[:, :, 0:4], cols[:, 1, :, :], ALU.add, ["ps%d" % b2, "collb"], ["sc"])
        self.tt("dve", sd4, p28[:, :, 4:8], cols[:, 4, :, :], ALU.subtract, ["ps%d" % b2, "colG"], ["sd"])
        self.act(cols[:, 6, :, :], sc4, AF.Exp, ["sc"], ["colk"])
        self.act(cols[:, 7, :, :], sd4, AF.Exp, ["sd"], ["colkd"])

        for c in range(8):
            sq = self.tmpb[c % 2]
            kq = "tmpb%d" % (c % 2)
            self.act(sq[:], self.qkv[:, c, :], AF.Square, ["qkv%d" % c], [kq])
            b = self.ps()
            pst = self.psum[b]
            self.mm(pst[:], onesb, sq[:], True, True, ["cstb", kq], ["ps%d" % b])
            rn = self.tmp[2 + c % 2]
            kr = "tmp%d" % (2 + c % 2)
            self.rsqrt(rn[:], pst[:], 2, 1.0, ["ps%d" % b], [kr])
            sc_ = (128.0 ** -0.5) if c < 4 else 1.0
            self.stt("pool", self.qkv[:, c, :], self.qkv[:, c, :], sc_, rn[:], ALU.mult, ALU.mult, ["qkv%d" % c, kr], ["qkv%d" % c])

        if self.pending_cc is not None:
            par = self.pending_cc
            self.pending_cc = None
            snd, gat = self.snd[par], self.gat[par]
            self.s.add("pool", lambda e, snd=snd, gat=gat: e.collective_compute(
                "AllGather", ALU.bypass, replica_groups=[[0, 1], [2, 3], [4, 5], [6, 7]],
                ins=[snd.ap().opt()], outs=[gat.ap().opt()]), ["snd%d" % par], ["gat%d" % par],
                dma=True, semkey="cc", inc=1)
        self.ps_n = 5
        self.psn = 0
        self.scan_gen = None
        for tb in range(4):
            self.deltanet_block(tb)
            if self.scan_gen is not None:
                for _ in self.scan_gen:
                    pass
            self.scan_gen = self.dn_scan(tb)
        for _ in self.scan_gen:
            pass
        self.scan_gen = None
        self.ps_n = 8
        if self.filler is not None:
            for _ in self.filler:
                pass
        self.filler = None

        for h in range(4):
            sq = self.tmpb[h % 2]
            kq = "tmpb%d" % (h % 2)
            self.act(sq[:], self.osb[:, h, :], AF.Square, ["osb%d" % h], [kq])
            b = self.ps()
            pst = self.psum[b]
            self.mm(pst[:], onesb, sq[:], True, True, ["cstb", kq], ["ps%d" % b])
            rn = self.tmp[2 + h % 2]
            kr = "tmp%d" % (2 + h % 2)
            self.rsqrt(rn[:], pst[:], 1, 1.0 / 128.0, ["ps%d" % b], [kr])
            self.stt("pool", rn[:], self.osb[:, h, :], pcol[:, NW:NW + 1], rn[:], ALU.mult, ALU.mult, ["osb%d" % h, kr, "pcol"], [kr])
            self.tt("pool", self.oab[:, h, :], rn[:], self.zs[:, h, :], ALU.mult, [kr, "zs%d" % h], ["oab%d" % h])

        for g in range(4):
            b = self.ps()
            pst = self.psum[b]
            for tb in range(4):
                self.mm(pst[:, tb * P:(tb + 1) * P], self.sgv[:, tb, g * P:(g + 1) * P], self.wsTb[:, g, :], True, True,
                        ["sgv%d" % tb, "wsTb"], ["ps%d" % b])
            t0 = self.tmp[g % 2]
            k0 = "tmp%d" % (g % 2)
            bias = self.prow[:, 1024 + g * P:1024 + (g + 1) * P]
            for tb in range(4):
                self.tt("dve", t0[:, tb * P:(tb + 1) * P], pst[:, tb * P:(tb + 1) * P], bias, ALU.add, ["ps%d" % b, "prow"], [k0])
            self.tt("pool", self.oab[:, 4 + g, :], t0[:], self.ug[:, g, :], ALU.mult, [k0, "ug%d" % g], ["oab%d" % (4 + g)])

        for j in range(2):
            slot = getw(B_AB + j)
            wt = self.wring[slot]
            for m in range(4):
                c = 4 * j + m
                ba_ = self.ps()
                bb_ = self.ps()
                pa, pb = self.psum[ba_], self.psum[bb_]
                for kc in range(4):
                    self.mm(pa[:], wt[:, kc, m * P:(m + 1) * P], self.oab[:, kc, :], kc == 0, kc == 3,
                            ["wr%d" % slot, "oab%d" % kc], ["ps%d" % ba_])
                for kc in range(4):
                    self.mm(pb[:], wt[:, 4 + kc, m * P:(m + 1) * P], self.oab[:, 4 + kc, :], kc == 0, kc == 3,
                            ["wr%d" % slot, "oab%d" % (4 + kc)], ["ps%d" % bb_])
                t0 = self.tmp[c % 2]
                k0 = "tmp%d" % (c % 2)
                t1 = self.tmp[2 + c % 2]
                k1 = "tmp%d" % (2 + c % 2)
                self.tt("dve", t0[:], pa[:], self.gates[:, c, :], ALU.mult, ["ps%d" % ba_, "gt%d" % c], [k0])
                self.tt("dve", t1[:], pb[:], self.gates[:, 8 + c, :], ALU.mult, ["ps%d" % bb_, "gt%d" % (8 + c)], [k1])
                self.tt("pool", actb[:, c, :], t0[:], t1[:], ALU.add, [k0, k1], ["ab%d" % c])

        self.proj_res_ln(B_OUT, getw, 8, actb, xT, "xT", y, "y", L1G, L1B, actb, "ab")
        if self.pp and s + 1 < self.n_steps:
            self.dma("sp", xT[:], self.xin[:, tok0 + T:tok0 + 2 * T].rearrange("(c p) t -> p c t", p=P), (),
                     ["xT%d" % c for c in range(8)], "d_x")

        for j in range(11):
            slot = getw(B_UP + j)
            wt = self.wring[slot]
            accs = []
            for m in range(4):
                ci = 4 * j + m
                b = self.ps()
                pst = self.psum[b]
                for kc in range(8):
                    self.mm(pst[:], wt[:, kc, m * P:(m + 1) * P], actb[:, kc, :], kc == 0, kc == 7,
                            ["wr%d" % slot, "ab%d" % kc], ["ps%d" % b])
                self.conv_evac(pst, b, ci, 3, self.halof, pcol[:, CF + 3 * ci:CF + 3 * ci + 3], "f")
                accs.append(self.lastacc)
                if m >= 2:
                    an = accs[m - 2]
                    bn = accs[m]
                    sa = self.tmp[m % 2]
                    ks = "tmp%d" % (m % 2)
                    self.act(sa[:], self.acc[an][:], AF.Silu, ["acc%d" % an], [ks])
                    hi = 2 * j + (m - 2)
                    self.tt("pool", self.hid[:, hi, :], sa[:], self.acc[bn][:], ALU.mult, [ks, "acc%d" % bn], ["qkv%d" % (hi // 2)])

        if self.pp:
            self.recv_next = (s + 1) if (2 <= s + 1 < self.n_steps) else None
            self.proj_res_ln(B_DN, getw, 22, self.hid, y, "y", y, "y", L2G, L2B, None, None)
            outbuf, outk_ = y, "y"
        else:
            self.proj_res_ln(B_DN, getw, 22, self.hid, y, "y", xT, "xT", L2G, L2B, None, None)
            outbuf, outk_ = xT, "xT"
        assert nxt[0] == NBLK and not pending
        if l == self.depth - 1:
            self.dma("sp", self.yout[:, tok0:tok0 + T].rearrange("(c p) t -> p c t", p=P), outbuf[:],
                     ["%s%d" % (outk_, c) for c in range(8)], ["yout"], "d_y")
        if self.pp:
            if s < self.n_steps - 2:
                par = s % 2
                self.dma("sp", self.snd[par].ap().rearrange("(c p) t -> p c t", p=P), outbuf[:],
                         ["%s%d" % (outk_, c) for c in range(8)], ["snd%d" % par], "d_snd")
                self.pending_cc = par
            if s < 2:
                kc_ = self.flg[:, 2 + s:3 + s]
                fl2 = lambda t: t[:].rearrange("p a b -> p (a b)")
                hq = ["haloq%d_%d" % (c, l) for c in range(12)]
                hf = ["halof%d_%d" % (c, l) for c in range(44)]
                self.ts("dve", fl2(self.haloq), fl2(self.haloq), kc_, None, ALU.mult, None, hq + ["flg"], hq)
                self.ts("dve", fl2(self.halof), fl2(self.halof), kc_, None, ALU.mult, None, hf + ["flg"], hf)
                self.ts("dve", fl2(self.S), fl2(self.S), kc_, None, ALU.mult, None, ["S", "flg"], ["S"])
                self.ts("dve", fl2(self.Sb), fl2(self.Sb), kc_, None, ALU.mult, None, ["Sb", "flg"], ["Sb"])

    def fill(self, every=2):
        sg = getattr(self, "scan_gen", None)
        if sg is not None:
            try:
                next(sg)
            except StopIteration:
                self.scan_gen = None
        if self.filler is None:
            return
        self.fill_ctr += 1
        if self.fill_ctr % every:
            return
        try:
            next(self.filler)
        except StopIteration:
            self.filler = None

    def gelu(self, src, b, out, wkeys):
        self.act(out, src, AF.Gelu_apprx_tanh, ["ps%d" % b], wkeys)

    def rsqrt(self, out, src, epsi, scale, r, w):
        self.act(out, src, AF.Sqrt, r + ["epsc"], w, bias=self.epsc[:, epsi:epsi + 1], scale=scale)
        self.s.add("dve", lambda e: e.reciprocal(out=out, in_=out), w, w)

    def conv_evac(self, pst, b, ci, K, halo, wcols, tag):
        n = self.rawn % 3
        self.rawn += 1
        raw = self.raw[n]
        kr = "raw%d" % n
        acc = self.acc[n]
        ka = "acc%d" % n
        hk = "halo%s%d_%d" % (tag, ci, self.cur_l)
        H = K - 1
        self.cp("act", raw[:, 4:4 + T], pst[:], ["ps%d" % b], [kr + "m"])
        self.cp("pool", raw[:, 4 - H:4], halo[:, ci, 4 - H:4], [hk], [kr + "h"])
        rk = [kr + "m", kr + "h"]
        self.act(acc[:], pst[:], AF.Copy, ["ps%d" % b, "pcol"], [ka], scale=wcols[:, K - 1:K])
        for jj in range(1, K - 1):
            sh = (K - 1) - jj
            self.stt("dve", acc[:], raw[:, 4 - sh:4 - sh + T], wcols[:, jj:jj + 1], acc[:], ALU.mult, ALU.add,
                     rk + ["pcol", ka], [ka])
        sh = K - 1
        self.ts("pool", raw[:, 4 - sh:4 - sh + T], raw[:, 4 - sh:4 - sh + T], wcols[:, 0:1], None, ALU.mult, None,
                rk + ["pcol"], rk) if False else None
        self.stt("dve", acc[:], raw[:, 4 - sh:4 - sh + T], wcols[:, 0:1], acc[:], ALU.mult, ALU.add,
                 rk + ["pcol", ka], [ka])
        self.cp("pool", halo[:, ci, 4 - H:4], raw[:, 4 + T - H:4 + T], rk, [hk])
        self.lastacc = n

    def proj_res_ln(self, blk0, getw, nkc, src, res, resk, out, outk, gcol, bcol, outb, outbk):
        pcol = self.pcol
        cstb = self.cstb
        onesb = cstb[:, C_ONE, :]
        srck = "ab" if src is self.actb else "hid"
        nparts = (nkc + 7) // 8
        r = out
        for j in range(2):
            banks = [self.ps() for _ in range(4)]
            for part in range(nparts):
                slot = getw(blk0 + j * nparts + part)
                wt = self.wring[slot]
                k0 = part * 8
                k1 = min(nkc, k0 + 8)
                for m in range(4):
                    pst = self.psum[banks[m]]
                    for kc in range(k0, k1):
                        sk = ("ab%d" % kc) if srck == "ab" else ("qkv%d" % (kc // 2))
                        self.mm(pst[:], wt[:, kc - k0, m * P:(m + 1) * P], src[:, kc, :], kc == 0, kc == nkc - 1,
                                ["wr%d" % slot, sk], ["ps%d" % banks[m]])
            for m in range(4):
                c = 4 * j + m
                pst = self.psum[banks[m]]
                self.stt("dve", r[:, c, :], res[:, c, :], ALPHA, pst[:], ALU.mult, ALU.add,
                         ["%s%d" % (resk, c), "ps%d" % banks[m]], ["%s%d" % (outk, c)])
        if getattr(self, "recv_next", None) is not None and src is self.hid:
            sn = self.recv_next
            self.recv_next = None
            g = self.gat[sn % 2]
            self.dma("sp", self.qkv[:, 0:8, :], g.ap()[0:D, :].rearrange("(c p) t -> p c t", p=P), ["gat%d" % (sn % 2)],
                     ["qkv%d" % c for c in range(8)], "d_rcv")
        bm = self.ps()
        bq = self.ps()
        pm, pq = self.psum[bm], self.psum[bq]
        for c in range(8):
            rb = self.lnb[:, c % 2, :]
            rq = self.lnb[:, 2 + c % 2, :]
            kb_, kq_ = "lnb%d" % (c % 2), "lnb%d" % (2 + c % 2)
            self.cp("pool", rb, r[:, c, :], ["%s%d" % (outk, c)], [kb_])
            self.act(rq, r[:, c, :], AF.Square, ["%s%d" % (outk, c)], [kq_])
            self.mm(pm[:], onesb, rb, c == 0, c == 7, ["cstb", kb_], ["ps%d" % bm])
            self.mm(pq[:], onesb, rq, c == 0, c == 7, ["cstb", kq_], ["ps%d" % bq])
        st = self.stat
        self.ts("dve", st[:, 0, :], pm[:], 1.0 / D, None, ALU.mult, None, ["ps%d" % bm], ["osb0"])
        self.tt("dve", st[:, 2, :], st[:, 0, :], st[:, 0, :], ALU.mult, ["osb0"], ["osb2"])
        self.stt("dve", st[:, 1, :], pq[:], 1.0 / D, st[:, 2, :], ALU.mult, ALU.subtract, ["ps%d" % bq, "osb2"], ["osb1"])
        self.rsqrt(st[:, 1, :], st[:, 1, :], 0, 1.0, ["osb1"], ["osb1"])
        self.stt("dve", st[:, 2, :], st[:, 0, :], -1.0, st[:, 1, :], ALU.mult, ALU.mult, ["osb0", "osb1"], ["osb2"])
        for c in range(8):
            t0 = self.tmp[c % 4]
            k0 = "tmp%d" % (c % 4)
            kk = "%s%d" % (outk, c)
            self.tt("dve", t0[:], r[:, c, :], st[:, 1, :], ALU.mult, [kk, "osb1"], [k0])
            self.tt("dve" if c % 2 == 0 else "pool", t0[:], t0[:], st[:, 2, :], ALU.add, [k0, "osb2"], [k0])
            gc_ = pcol[:, gcol + c:gcol + c + 1]
            bc_ = pcol[:, bcol + c:bcol + c + 1]
            self.act(out[:, c, :], t0[:], AF.Identity, [k0, "pcol"], [kk], bias=bc_, scale=gc_)
            if outb is not None:
                self.act(outb[:, c, :], t0[:], AF.Identity, [k0, "pcol"], ["%s%d" % (outbk, c)], bias=bc_, scale=gc_)

    def deltanet_block(self, tb):
        cst, cstb = self.cst, self.cstb
        ident = cst[:, C_ID, :]
        identb = cstb[:, C_ID, :]
        triu = cst[:, C_TRIU, :]
        cols = self.cols
        dn = self.dn
        qkv = self.qkv
        blk = slice(tb * P, (tb + 1) * P)
        kcols = ["colb", "collb", "colg", "colng", "colG", "colk", "colkd"]
        pr = tb % 2
        d2 = self.dn2[pr]

        def bc(kind, h):
            return cols[:, kind, tb, h:h + 1].to_broadcast([P, P])

        bU, bA, bL, bG = self.ps(), self.ps(), self.ps(), self.ps()
        pU, pA, pL, pG = self.psum[bU], self.psum[bA], self.psum[bL], self.psum[bG]
        for h in range(4):
            hs = slice(h * P, (h + 1) * P)
            self.mm(pU[:, hs], bc(2, h), triu, True, False, kcols + ["cst"], ["ps%d" % bU])
            self.mm(pU[:, hs], triu, bc(3, h), False, False, kcols + ["cst"], ["ps%d" % bU])
            self.mm(pU[:, hs], bc(1, h), ident, False, False, kcols + ["cst"], ["ps%d" % bU])
            self.mm(pU[:, hs], identb, cstb[:, C_MUS, :], False, True, ["cstb"], ["ps%d" % bU])
            self.mm(pA[:, hs], bc(2, h), triu, True, False, kcols + ["cst"], ["ps%d" % bA])
            self.mm(pA[:, hs], triu, bc(3, h), False, False, kcols + ["cst"], ["ps%d" % bA])
            self.mm(pA[:, hs], identb, cstb[:, C_MUI, :], False, True, ["cstb"], ["ps%d" % bA])
            self.mm(pL[:, hs], triu, bc(2, h), True, False, kcols + ["cst"], ["ps%d" % bL])
            self.mm(pL[:, hs], bc(3, h), triu, False, False, kcols + ["cst"], ["ps%d" % bL])
            self.mm(pL[:, hs], ident, bc(1, h), False, False, kcols + ["cst"], ["ps%d" % bL])
            self.mm(pL[:, hs], identb, cstb[:, C_NLS, :], False, True, ["cstb"], ["ps%d" % bL])
            self.mm(pG[:, hs], bc(2, h), triu, True, True, kcols + ["cst"], ["ps%d" % bG])

        def fl(t):
            return t[:].rearrange("p a b -> p (a b)")

        self.act(fl(dn["DUs"]), pU[:], AF.Exp, ["ps%d" % bU], ["DUs"])
        self.act(fl(dn["DUi"]), pA[:], AF.Exp, ["ps%d" % bA], ["DUi"])
        self.act(fl(dn["DLs"]), pL[:], AF.Exp, ["ps%d" % bL], ["DLs"])
        self.act(fl(dn["EG"]), pG[:], AF.Exp, ["ps%d" % bG], ["EG"])

        qkb = self.qkb
        self.cp("act", qkb[:], qkv[:, 0:8, blk], ["qkv%d" % i for i in range(8)], ["qkb"])
        bK, bQ = self.ps(), self.ps()
        pK, pQ = self.psum[bK], self.psum[bQ]
        for h in range(4):
            hs = slice(h * P, (h + 1) * P)
            self.mm(pK[:, hs], qkb[:, 4 + h, :], qkb[:, 4 + h, :], True, True, ["qkb"], ["ps%d" % bK])
            self.mm(pQ[:, hs], qkb[:, 4 + h, :], qkb[:, h, :], True, True, ["qkb"], ["ps%d" % bQ])
        self.fill(1)
        self.stt("dve", fl(dn["Y"]), pK[:], -1.0, fl(dn["DUs"]), ALU.mult, ALU.mult, ["ps%d" % bK, "DUs"], ["Y"])
        self.stt("dve", fl(dn["YT"]), pK[:], -1.0, fl(dn["DLs"]), ALU.mult, ALU.mult, ["ps%d" % bK, "DLs"], ["YT"])
        self.tt("dve", fl(d2["Aqk"]), pQ[:], fl(dn["DUi"]), ALU.mult, ["ps%d" % bQ, "DUi"], ["Aqk%d" % pr])
        self.tt("dve", dn["Pm"][:], dn["Y"][:], cstb[:, C_ID:C_ID + 1, :].to_broadcast([P, 4, P]), ALU.add, ["Y", "cstb"], ["Pm"])
        self.tt("pool", d2["qg"][:], qkv[:, 0:4, blk], dn["EG"][:], ALU.mult, ["qkv0", "qkv1", "qkv2", "qkv3", "EG"], ["qg%d" % pr])
        self.cp("pool", d2["gl"][:, :, 0:1], dn["EG"][:, :, 63:64], ["EG"], ["gl%d" % pr])
        self.cp("pool", d2["gl"][:, :, 1:2], dn["EG"][:, :, 127:128], ["EG"], ["gl%d" % pr])

        bT, bV = self.ps(), self.ps()
        pT, pV = self.psum[bT], self.psum[bV]
        for h in range(4):
            hs = slice(h * P, (h + 1) * P)
            self.tr(pT[:, hs], qkv[:, 4 + h, blk], ident, ["qkv%d" % (4 + h), "cst"], ["ps%d" % bT])
            self.tr(pV[:, hs], qkv[:, 8 + h, blk], ident, ["qkv%d" % (8 + h), "cst"], ["ps%d" % bV])
        self.fill(1)

        def cb(kind):
            return cols[:, kind, tb, :].rearrange("p (h o) -> p h o", o=1).to_broadcast([P, 4, P])

        p3 = lambda ap: ap.rearrange("p (a b) -> p a b", a=4)
        self.tt("dve", dn["Kbg"][:], p3(pT[:]), cb(6), ALU.mult, ["ps%d" % bT] + kcols, ["Kbg"])
        self.tt("dve", d2["kd"][:], p3(pT[:]), cb(7), ALU.mult, ["ps%d" % bT] + kcols, ["kd%d" % pr])
        self.tt("dve", dn["Vb"][:], p3(pV[:]), cb(0), ALU.mult, ["ps%d" % bV] + kcols, ["Vb"])

        Z, ZT, Z2, ZT2 = "Y", "YT", "Z", "ZT"
        for lev in range(1, 6):
            last = lev == 5
            bz, bzt = self.ps(), self.ps()
            pz, pzt = self.psum[bz], self.psum[bzt]
            for h in range(4):
                hs = slice(h * P, (h + 1) * P)
                if not last:
                    self.mm(pz[:, hs], dn[ZT][:, h, :], dn[Z][:, h, :], True, True, [Z, ZT], ["ps%d" % bz])
                self.mm(pzt[:, hs], dn[Z][:, h, :], dn[ZT][:, h, :], True, True, [Z, ZT], ["ps%d" % bzt])
            self.fill(2)
            if not last:
                self.cp("dve", fl(dn[Z2]), pz[:], ["ps%d" % bz], [Z2])
            self.cp("act", fl(dn[ZT2]), pzt[:], ["ps%d" % bzt], [ZT2])
            bp = self.ps()
            pp = self.psum[bp]
            for h in range(4):
                hs = slice(h * P, (h + 1) * P)
                self.mm(pp[:, hs], dn[ZT2][:, h, :], dn["Pm"][:, h, :], True, True, [ZT2, "Pm"], ["ps%d" % bp])
            self.fill(2)
            self.tt("dve", fl(dn["Pm"]), fl(dn["Pm"]), pp[:], ALU.add, ["Pm", "ps%d" % bp], ["Pm"])
            if Z == "Y":
                Z, ZT, Z2, ZT2 = "Z", "ZT", "Z2", "ZT2"
            else:
                Z, ZT, Z2, ZT2 = Z2, ZT2, Z, ZT

        bW, bUu = self.ps(), self.ps()
        pW, pUu = self.psum[bW], self.psum[bUu]
        for h in range(4):
            hs = slice(h * P, (h + 1) * P)
            self.mm(pW[:, hs], dn["Kbg"][:, h, :], dn["Pm"][:, h, :], True, True, ["Kbg", "Pm"], ["ps%d" % bW])
            self.mm(pUu[:, hs], dn["Pm"][:, h, :], dn["Vb"][:, h, :], True, True, ["Pm", "Vb"], ["ps%d" % bUu])
        self.fill(1)
        self.cp("act", fl(d2["WT"]), pW[:], ["ps%d" % bW], ["WT%d" % pr])
        self.cp("act", fl(d2["U"]), pUu[:], ["ps%d" % bUu], ["U%d" % pr])

    def dn_scan(self, tb):
        pr = tb % 2
        d2 = self.dn2[pr]
        blk = slice(tb * P, (tb + 1) * P)
        fl = lambda t: t[:].rearrange("p a b -> p (a b)")
        p3 = lambda ap: ap.rearrange("p (a b) -> p a b", a=4)
        bO, bws, bs = 5, 6, 7
        pO, pws, pS = self.psum[bO], self.psum[bws], self.psum[bs]
        S = self.S
        Sb = self.Sb
        for cc in range(2):
            rows = slice(cc * 64, (cc + 1) * 64)
            for h in range(4):
                hs = slice(h * P, (h + 1) * P)
                self.mm(pws[:, hs], d2["WT"][:, h, :], Sb[:, h, :], True, True, ["WT%d" % pr, "Sb"], ["ps%d" % bws])
            yield
            self.tt("dve", d2["un"][rows, :, :], d2["U"][rows, :, :], p3(pws[rows, :]), ALU.subtract,
                    ["U%d" % pr, "ps%d" % bws], ["un%d" % pr])
            yield
            for h in range(4):
                oc = slice(h * P + cc * 64, h * P + (cc + 1) * 64)
                self.mm(pO[:, oc], Sb[:, h, :], d2["qg"][:, h, rows], True, False, ["Sb", "qg%d" % pr], ["ps%d" % bO])
                self.mm(pO[:, oc], d2["un"][rows, h, :], d2["Aqk"][rows, h, rows], False, True,
                        ["un%d" % pr, "Aqk%d" % pr], ["ps%d" % bO])
                hs = slice(h * P, (h + 1) * P)
                self.mm(pS[:, hs], d2["kd"][rows, h, :], d2["un"][rows, h, :], True, True,
                        ["kd%d" % pr, "un%d" % pr], ["ps%d" % bs])
            glb = d2["gl"][:, :, cc:cc + 1].to_broadcast([P, 4, P])
            self.tt("pool", S[:], S[:], glb, ALU.mult, ["S", "gl%d" % pr], ["S"])
            yield
            self.tt("dve", fl(S), fl(S), pS[:], ALU.add, ["S", "ps%d" % bs], ["S"])
            self.cp("act", fl(Sb), fl(S), ["S"], ["Sb"])
            yield
        self.cp("act", self.osb[:, :, blk], p3(pO[:]), ["ps%d" % bO], ["osb0", "osb1", "osb2", "osb3"])


def _consts():
    idx = np.arange(P)
    same = (idx[:, None] // 64) == (idx[None, :] // 64)
    c = np.zeros((P, 8, P), np.float32)
    c[:, C_ID, :] = np.eye(P, dtype=np.float32)
    c[:, C_TRIU, :] = ((idx[:, None] <= idx[None, :]) & same)
    c[:, C_BLK, :] = same
    c[:, C_MUS, :] = np.where((idx[None, :] > idx[:, None]) & same, 0.0, -BIG)
    c[:, C_MUI, :] = np.where((idx[None, :] >= idx[:, None]) & same, 0.0, -BIG)
    c[:, C_NLS, :] = np.where((idx[None, :] < idx[:, None]) & same, 0.0, -BIG)
    c[:, C_SPU, :] = (idx[:, None] <= idx[None, :])
    c[:, C_ONE, :] = 1.0
    return np.ascontiguousarray(c.reshape(P, 8 * P))


def _blockify(w):
    K = w.shape[0]
    out = np.zeros((P, 8, 512), np.float32)
    nk = K // P
    out[:, :nk, :] = w.reshape(nk, P, 512).transpose(1, 0, 2)
    return out


def prep_layer(inp, l):
    w_in = np.asarray(inp["w_in"][l])
    qkvz = w_in[:, 0:2048]
    ba = w_in[:, 2048:2056]
    u = w_in[:, 2056:2568]
    sv = w_in[:, 2568:3080]
    gates = w_in[:, 3080:5128]
    fm = np.concatenate([qkvz, u, gates], axis=1)
    blocks = [_blockify(fm[:, j * 512:(j + 1) * 512]) for j in range(9)]
    blocks.append(_blockify(sv))
    wa = np.asarray(inp["w_branch_a"][l])
    wb = np.asarray(inp["w_branch_b"][l])
    for j in range(2):
        blocks.append(_blockify(np.concatenate([wa[:, j * 512:(j + 1) * 512], wb[:, j * 512:(j + 1) * 512]], axis=0)))
    wo = np.asarray(inp["w_out"][l])
    for j in range(2):
        blocks.append(_blockify(wo[:, j * 512:(j + 1) * 512]))
    wu = np.asarray(inp["w_up"][l])
    for j in range(11):
        blocks.append(_blockify(np.concatenate([wu[:, j * 256:(j + 1) * 256], wu[:, FFN + j * 256:FFN + (j + 1) * 256]], axis=1)))
    wd = np.asarray(inp["w_down"][l])
    for j in range(2):
        for part in range(3):
            k0 = part * 1024
            k1 = min(FFN, k0 + 1024)
            blocks.append(_blockify(wd[k0:k1, j * 512:(j + 1) * 512]))
    wblk = np.stack(blocks).reshape(NBLK, P, 4096)
    wba = np.ascontiguousarray(ba.reshape(8, P, 8).transpose(1, 0, 2).reshape(P, 64))
    pcol = np.zeros((P, NPC), np.float32)
    cq = np.asarray(inp["conv_qkv"][l])
    pcol[:, CQ:CQ + 48] = cq.reshape(4, 12, P).transpose(2, 1, 0).reshape(P, 48)
    cf = np.asarray(inp["conv_ffn"][l])
    cfa = cf[:, :FFN].reshape(3, 22, P)
    cfb = cf[:, FFN:].reshape(3, 22, P)
    cfo = np.zeros((P, 44, 3), np.float32)
    for j in range(11):
        for m in range(2):
            cfo[:, 4 * j + m, :] = cfa[:, 2 * j + m, :].T
            cfo[:, 4 * j + 2 + m, :] = cfb[:, 2 * j + m, :].T
    pcol[:, CF:CF + 132] = cfo.reshape(P, 132)
    pcol[:, NW] = np.asarray(inp["dn_norm_w"][l])
    pcol[:, L1G:L1G + 8] = np.asarray(inp["ln1_g"][l]).reshape(8, P).T
    pcol[:, L1B:L1B + 8] = np.asarray(inp["ln1_b"][l]).reshape(8, P).T
    pcol[:, L2G:L2G + 8] = np.asarray(inp["ln2_g"][l]).reshape(8, P).T
    pcol[:, L2B:L2B + 8] = np.asarray(inp["ln2_b"][l]).reshape(8, P).T
    pcol[:, ALOG:ALOG + 4] = np.asarray(inp["a_log"][l])[None, :]
    pcol[:, DTB:DTB + 4] = np.asarray(inp["dt_bias"][l])[None, :]
    prow = np.zeros((P, 1536), np.float32)
    prow[:, 0:512] = np.asarray(inp["sg_ln_g"][l])[None, :]
    prow[:, 512:1024] = np.asarray(inp["sg_ln_b"][l])[None, :]
    prow[:, 1024:1536] = np.asarray(inp["b_spatial"][l]).reshape(1, 512)
    ws = np.asarray(inp["w_spatial"][l])
    wsT = np.ascontiguousarray(ws.transpose(2, 0, 1).reshape(P, 512))
    return {"wblk": np.ascontiguousarray(wblk), "wba": wba, "pcol": pcol, "prow": prow, "wsT": wsT, "cst": _consts()}


_NC_CACHE = {}


def get_nc(n_steps, depth, pp=False):
    if (n_steps, depth, pp) not in _NC_CACHE:
        _NC_CACHE[(n_steps, depth, pp)] = Builder(n_steps, depth, pp).build()
    return _NC_CACHE[(n_steps, depth, pp)]


def run_pp(xT_list, lays, n_tiles):
    n_steps = n_tiles + 2
    nc = get_nc(n_steps, 1, True)
    in_maps = []
    for b, xT in enumerate(xT_list):
        for role in range(2):
            lay = lays[role]
            m = {"cst": lay["cst"]}
            for k in ("wblk", "wba", "pcol", "prow", "wsT"):
                m["%s0" % k] = lay[k]
            xin = np.zeros((D, n_steps * T), np.float32)
            flag = np.ones((P, 2 + n_steps), np.float32)
            if role == 0:
                xin[:, :n_tiles * T] = xT
                flag[:, 0] = 0.0
            else:
                flag[:, 2:4] = 0.0
            m["xin"] = xin
            m["flag"] = flag
            in_maps.append(m)
    res = run_bass_kernel_spmd(nc, in_maps, core_ids=list(range(len(in_maps))))
    return [res.results[2 * b + 1]["yout"][:, 2 * T:] for b in range(len(xT_list))]


def run_layers(xT_list, lays, n_steps):
    nc = get_nc(n_steps, len(lays))
    base = {"cst": lays[0]["cst"]}
    for l, lay in enumerate(lays):
        for k in ("wblk", "wba", "pcol", "prow", "wsT"):
            base["%s%d" % (k, l)] = lay[k]
    in_maps = []
    for xT in xT_list:
        m = dict(base)
        m["xin"] = np.ascontiguousarray(xT, dtype=np.float32)
        in_maps.append(m)
    res = run_bass_kernel_spmd(nc, in_maps, core_ids=list(range(len(xT_list))))
    return [r["yout"] for r in res.results]


def kernel(**inputs):
    x = np.asarray(inputs["x"], dtype=np.float32)
    n_steps = SEQ // T
    cur = [np.ascontiguousarray(x[b].T) for b in range(BATCH)]
    lays = [prep_layer(inputs, l) for l in range(DEPTH)]
    cur = run_pp(cur, lays, n_steps)
    out = np.stack([c.T for c in cur]).astype(np.float32)
    return out
```
